# Optimizing a Trainium2 kernel written in Bass

```python
import math
import jax, jax.numpy as jnp
from jax import lax
import numpy as np

D_MODEL = 2048
BATCH = 4
SEQ = 2048
DEPTH = 1
DEC_BATCH = 128
DEC_SEQ = 4
PAST_LEN = 16384
PAGE_SIZE = 128

MIX_WIDTH = D_MODEL
POOL_WIDTH = MIX_WIDTH // 2
RET_WIDTH = MIX_WIDTH - POOL_WIDTH
POOL_WINDOWS = (2, 4, 8, 16)
N_POOL_GROUPS = len(POOL_WINDOWS)
POOL_GROUP_DIM = POOL_WIDTH // N_POOL_GROUPS
POOL_BUF = max(POOL_WINDOWS) - 1
RET_HEADS = 4
RET_HEAD_DIM = RET_WIDTH // RET_HEADS
RET_CHUNK = 128
ROPE_BASE = 10000.0
D_FF = 5632
N_MOD = 9
EPS = 1e-6
IN_COLS = POOL_WIDTH + 4 * RET_WIDTH

kernel_name = "hymba_pool_retnet_macaron_adaln_step"


def rmsnorm(x, gain):
    x32 = x.astype(jnp.float32)
    y = x32 * lax.rsqrt(jnp.mean(x32 * x32, axis=-1, keepdims=True) + EPS)
    return (y * gain.astype(jnp.float32)).astype(x.dtype)


def modulate(h, shift, scale):
    return h * (1 + scale[:, None, :]) + shift[:, None, :]


def swiglu(h, w_gate, w_up, w_down):
    return (jax.nn.silu(h @ w_gate) * (h @ w_up)) @ w_down


def rotary(x, pos):
    half = x.shape[-1] // 2
    inv = ROPE_BASE ** (-jnp.arange(half, dtype=jnp.float32) / half)
    ang = pos[:, None] * inv[None, :]
    cos = jnp.cos(ang)[None, :, None, :]
    sin = jnp.sin(ang)[None, :, None, :]
    x1, x2 = x[..., :half], x[..., half:]
    return jnp.concatenate([x1 * cos - x2 * sin, x2 * cos + x1 * sin], axis=-1)


def pool_mixer(u, buf, pos0, pool_w, pool_scale):
    B, T, _ = u.shape
    u32 = u.astype(jnp.float32)
    ue = jnp.concatenate([buf.astype(jnp.float32), u32], axis=1)
    cs = jnp.cumsum(ue, axis=1)
    cs = jnp.concatenate([jnp.zeros_like(cs[:, :1]), cs], axis=1)
    end = POOL_BUF + 1
    pos = jnp.arange(T, dtype=jnp.float32) + pos0
    parts = []
    for g, w in enumerate(POOL_WINDOWS):
        sl = slice(g * POOL_GROUP_DIM, (g + 1) * POOL_GROUP_DIM)
        wsum = cs[:, end:end + T, sl] - cs[:, end - w:end - w + T, sl]
        count = jnp.minimum(pos + 1.0, float(w))
        parts.append(wsum / count[None, :, None] - u32[:, :, sl])
    m = jnp.stack(parts, axis=2)
    out = jnp.einsum('btgc,gcd->btgd', m, pool_w.astype(jnp.float32)).reshape(B, T, POOL_WIDTH)
    out = out * pool_scale.astype(jnp.float32)
    new_buf = ue[:, -POOL_BUF:]
    return out.astype(u.dtype), new_buf.astype(buf.dtype)


def retention(q, k, v, s0, pos0):
    B, T, _ = q.shape
    C = math.gcd(T, RET_CHUNK)
    n = T // C
    pos = jnp.arange(T, dtype=jnp.float32) + pos0

    def heads(t):
        return t.astype(jnp.float32).reshape(B, T, RET_HEADS, RET_HEAD_DIM)

    qh = rotary(heads(q), pos)
    kh = rotary(heads(k), pos) * (RET_HEAD_DIM ** -0.5)
    vh = heads(v)

    def chunks(t):
        return t.reshape(B, n, C, RET_HEADS, RET_HEAD_DIM).transpose(1, 0, 3, 2, 4)

    log_g = jnp.log(1.0 - jnp.power(2.0, -5.0 - jnp.arange(RET_HEADS, dtype=jnp.float32)))
    idx = jnp.arange(C, dtype=jnp.float32)
    diff = idx[:, None] - idx[None, :]
    intra = jnp.where(diff[None] >= 0, jnp.exp(jnp.maximum(diff, 0.0)[None] * log_g[:, None, None]), 0.0)
    q_dec = jnp.exp((idx + 1.0)[None, :] * log_g[:, None])
    k_dec = jnp.exp((C - 1.0 - idx)[None, :] * log_g[:, None])
    c_dec = jnp.exp(C * log_g)

    def step(S, xs):
        qc, kc, vc = xs
        scores = jnp.einsum('bhnd,bhmd->bhnm', qc, kc) * intra[None]
        o = jnp.einsum('bhnm,bhmv->bhnv', scores, vc)
        o = o + jnp.einsum('bhnd,bhdv->bhnv', qc, S) * q_dec[None, :, :, None]
        S = S * c_dec[None, :, None, None] + jnp.einsum('bhmd,bhmv->bhdv', kc * k_dec[None, :, :, None], vc)
        return S, o

    S, o = lax.scan(step, s0.astype(jnp.float32), (chunks(qh), chunks(kh), chunks(vh)))
    o = o.transpose(1, 0, 3, 2, 4).reshape(B, T, RET_HEADS, RET_HEAD_DIM)
    o = o * lax.rsqrt(jnp.mean(o * o, axis=-1, keepdims=True) + EPS)
    return o.reshape(B, T, RET_WIDTH), S.astype(s0.dtype)


def layer(x, c, pool_buf, ret_state, pos0, lw):
    (ada_w, ada_b, norm_ffn1, ffn1_w_gate, ffn1_w_up, ffn1_w_down, norm_mix, w_in, pool_w,
     pool_scale, w_out, norm_ffn2, ffn2_w_gate, ffn2_w_up, ffn2_w_down) = lw
    mods = jax.nn.silu(c) @ ada_w + ada_b
    sh1, sc1, gt1, sh2, sc2, gt2, sh3, sc3, gt3 = jnp.split(mods, N_MOD, axis=-1)
    h = modulate(rmsnorm(x, norm_ffn1), sh1, sc1)
    x = x + 0.5 * gt1[:, None, :] * swiglu(h, ffn1_w_gate, ffn1_w_up, ffn1_w_down)
    h = modulate(rmsnorm(x, norm_mix), sh2, sc2)
    proj = h @ w_in
    u, q, k, v, g = jnp.split(proj, [POOL_WIDTH, POOL_WIDTH + RET_WIDTH, POOL_WIDTH + 2 * RET_WIDTH,
                                     POOL_WIDTH + 3 * RET_WIDTH], axis=-1)
    pool_out, new_buf = pool_mixer(u, pool_buf, pos0, pool_w, pool_scale)
    ret_out, new_state = retention(q, k, v, ret_state, pos0)
    ret_out = (jax.nn.silu(g.astype(jnp.float32)) * ret_out).astype(x.dtype)
    mix = jnp.concatenate([pool_out, ret_out], axis=-1) @ w_out
    x = x + gt2[:, None, :] * mix
    h = modulate(rmsnorm(x, norm_ffn2), sh3, sc3)
    x = x + 0.5 * gt3[:, None, :] * swiglu(h, ffn2_w_gate, ffn2_w_up, ffn2_w_down)
    return x, new_buf, new_state


def trunk(x, c, pool_state, ret_state, pos0, weights, norm_final):
    new_pool, new_ret = [], []
    for l in range(DEPTH):
        lw = tuple(w[l] for w in weights)
        x, pb, rs = layer(x, c, pool_state[l], ret_state[l], pos0, lw)
        new_pool.append(pb)
        new_ret.append(rs)
    y = rmsnorm(x, norm_final)
    return y, jnp.stack(new_pool), jnp.stack(new_ret)


def setup_inputs(seed: int = 0) -> dict:
    key = jax.random.key(seed)
    ks = jax.random.split(key, 24)
    f = jnp.float32
    D, F = D_MODEL, D_FF

    def nrm(k, shape, scale):
        return jax.random.normal(k, shape, f) * scale

    def gain(k, shape):
        return 1.0 + 0.05 * jax.random.normal(k, shape, f)

    return {
        "x_prompt": nrm(ks[0], (BATCH, SEQ, D), 1.0),
        "x_sample": nrm(ks[1], (DEC_BATCH, DEC_SEQ, D), 1.0),
        "c_prompt": nrm(ks[2], (BATCH, D), 1.0),
        "c_sample": nrm(ks[3], (DEC_BATCH, D), 1.0),
        "state_pool": nrm(ks[4], (DEPTH, DEC_BATCH, POOL_BUF, POOL_WIDTH), 1.0),
        "state_ret": nrm(ks[5], (DEPTH, DEC_BATCH, RET_HEADS, RET_HEAD_DIM, RET_HEAD_DIM), 0.5),
        "ada_w": nrm(ks[6], (DEPTH, D, N_MOD * D), 0.3 * D ** -0.5),
        "ada_b": nrm(ks[7], (DEPTH, N_MOD * D), 0.01),
        "norm_ffn1": gain(ks[8], (DEPTH, D)),
        "ffn1_w_gate": nrm(ks[9], (DEPTH, D, F), D ** -0.5),
        "ffn1_w_up": nrm(ks[10], (DEPTH, D, F), D ** -0.5),
        "ffn1_w_down": nrm(ks[11], (DEPTH, F, D), F ** -0.5),
        "norm_mix": gain(ks[12], (DEPTH, D)),
        "w_in": nrm(ks[13], (DEPTH, D, IN_COLS), D ** -0.5),
        "pool_w": nrm(ks[14], (DEPTH, N_POOL_GROUPS, POOL_GROUP_DIM, POOL_GROUP_DIM), POOL_GROUP_DIM ** -0.5),
        "pool_scale": gain(ks[15], (DEPTH, POOL_WIDTH)),
        "w_out": nrm(ks[16], (DEPTH, MIX_WIDTH, D), MIX_WIDTH ** -0.5),
        "norm_ffn2": gain(ks[17], (DEPTH, D)),
        "ffn2_w_gate": nrm(ks[18], (DEPTH, D, F), D ** -0.5),
        "ffn2_w_up": nrm(ks[19], (DEPTH, D, F), D ** -0.5),
        "ffn2_w_down": nrm(ks[20], (DEPTH, F, D), F ** -0.5),
        "norm_final": gain(ks[21], (D,)),
    }


def reference(x_prompt, x_sample, c_prompt, c_sample, state_pool, state_ret,
              ada_w, ada_b, norm_ffn1, ffn1_w_gate, ffn1_w_up, ffn1_w_down,
              norm_mix, w_in, pool_w, pool_scale, w_out,
              norm_ffn2, ffn2_w_gate, ffn2_w_up, ffn2_w_down, norm_final):
    weights = (ada_w, ada_b, norm_ffn1, ffn1_w_gate, ffn1_w_up, ffn1_w_down, norm_mix, w_in,
               pool_w, pool_scale, w_out, norm_ffn2, ffn2_w_gate, ffn2_w_up, ffn2_w_down)
    bp = x_prompt.shape[0]
    pool0 = jnp.zeros((DEPTH, bp, POOL_BUF, POOL_WIDTH), state_pool.dtype)
    ret0 = jnp.zeros((DEPTH, bp, RET_HEADS, RET_HEAD_DIM, RET_HEAD_DIM), state_ret.dtype)
    y_prompt, pool_prompt, ret_prompt = trunk(x_prompt, c_prompt, pool0, ret0, 0, weights, norm_final)
    y_sample, pool_sample, ret_sample = trunk(x_sample, c_sample, state_pool, state_ret, PAST_LEN,
                                              weights, norm_final)
    return (y_prompt, y_sample, pool_prompt, ret_prompt, pool_sample, ret_sample)
```

```python
import numpy as np
from contextlib import ExitStack
import concourse.bass as bass
import concourse.mybir as mybir
from concourse.bass_utils import run_bass_kernel_spmd

F32 = mybir.dt.float32
BF16 = mybir.dt.bfloat16
ALU = mybir.AluOpType
AF = mybir.ActivationFunctionType

D = 2048
DFF = 5632
NKT = D // 128
NFT = DFF // 128
NMOD = 9
INC = 5120
NB = 1
NPT = 8
TPB = 1024
NSB = 16
NS = 64
TB = TPB + NS
NCORE = 8
NSC = 16
EPS = 1e-6
FG = 4
NG = NFT // FG
GAM = [1.0 - 2.0 ** (-5.0 - h) for h in range(4)]


def dap(t, offset, pat):
    return bass.AP(t.tensor, offset, pat)


class Reg:
    __slots__ = ("name", "w", "r")

    def __init__(self, name):
        self.name = name
        self.w = None
        self.r = {}


class Bld:
    def __init__(self, nc, es):
        self.nc = nc
        self.E = {"pe": nc.tensor, "act": nc.scalar, "dve": nc.vector, "pool": nc.gpsimd, "sp": nc.sync}
        self.sem, self.cnt = {}, {}
        self.waited = {e: {} for e in self.E}
        for e in self.E:
            self.sem[e] = es.enter_context(nc.semaphore("s_" + e))
            self.cnt[e] = 0
        self.dsem, self.dpos = {}, {}
        for q, n in (("sp", 24), ("pool", 24), ("act", 8)):
            self.dsem[q] = [[es.enter_context(nc.semaphore("d_%s%d" % (q, i))), 0] for i in range(n)]
            self.dpos[q] = 0
        self.regs = []

    def reg(self, name):
        r = Reg(name)
        self.regs.append(r)
        return r

    def _wait(self, e, sem, val):
        k = id(sem)
        if self.waited[e].get(k, 0) >= val:
            return
        self.E[e].wait_ge(sem, val)
        self.waited[e][k] = val

    def _deps(self, reads, writes):
        evs = []
        for r in reads:
            if r.w is not None:
                evs.append(r.w)
        for w in writes:
            if w.w is not None:
                evs.append(w.w)
            evs.extend(w.r.values())
        return evs

    def _record(self, ev, reads, writes):
        k = id(ev[0])
        for r in reads:
            o = r.r.get(k)
            if o is None or o[1] < ev[1]:
                r.r[k] = ev
        for w in writes:
            w.w = ev
            w.r = {}

    def op(self, e, fn, reads=(), writes=(), inc=True):
        for ev in self._deps(reads, writes):
            if ev[2] == "pe" and e == "pe":
                continue
            self._wait(e, ev[0], ev[1])
        ins = fn(self.E[e])
        if inc:
            ins.then_inc(self.sem[e], 1)
            self.cnt[e] += 1
            ev = (self.sem[e], self.cnt[e], e)
        else:
            ev = (self.sem[e], self.cnt[e] + 1, e)
        self._record(ev, reads, writes)
        return ev

    def dma(self, q, out, in_, reads=(), writes=()):
        for ev in self._deps(reads, writes):
            self._wait(q, ev[0], ev[1])
        pool = self.dsem[q]
        slot = pool[self.dpos[q] % len(pool)]
        self.dpos[q] += 1
        if slot[1] > 0:
            self._wait(q, slot[0], slot[1])
        ins = self.E[q].dma_start(out=out, in_=in_)
        slot[1] += 16
        ins.then_inc(slot[0], 16)
        ev = (slot[0], slot[1], "dma")
        self._record(ev, reads, writes)
        return ev

    def barrier(self):
        evs = [(self.sem[e], self.cnt[e]) for e in self.E if self.cnt[e] > 0]
        for q in self.dsem:
            for s in self.dsem[q]:
                if s[1] > 0:
                    evs.append((s[0], s[1]))
        for e in self.E:
            for sem, val in evs:
                self._wait(e, sem, val)
        for r in self.regs:
            r.w = None
            r.r = {}


class Tile:
    _uid = [0]

    def __init__(self, b, es, name, shape, dtype, nreg=1, psum=False):
        Tile._uid[0] += 1
        name = "%s_%d" % (name, Tile._uid[0])
        if psum:
            self.t = es.enter_context(b.nc.psum_tensor(name, list(shape), dtype))
        else:
            self.t = es.enter_context(b.nc.sbuf_tensor(name, list(shape), dtype))
        self.r = [b.reg("%s_%d" % (name, i)) for i in range(nreg)]

    def __getitem__(self, k):
        return self.t[k]


class View:
    def __init__(self, ap, regs):
        self.t = ap
        self.r = regs

    def __getitem__(self, k):
        return self.t[k]


def build_program(stop_after=None):
    nc = bass.Bass("TRN2", target_bir_lowering=False)

    def din(name, shape, dt=F32):
        return nc.dram_tensor(name, list(shape), dt, kind="ExternalInput").ap()

    def dout(name, shape, dt=F32):
        return nc.dram_tensor(name, list(shape), dt, kind="ExternalOutput").ap()

    xp = din("xp", [NB * TPB, D])
    xp_pre = din("xp_pre", [TPB, D])
    flag = din("flag", [128, 1])
    xs = din("xs", [NB * NS, D])
    cT = din("cT", [D, 1 + NSC])
    spool = din("spool", [NSC, 15, 1024])
    sret = din("sret", [NSC, 4, 256, 256])
    ada_w = din("ada_w", [3 * D // 512, 128, NKT, 512])
    ada_wb = din("ada_wb", [6 * D // 128, 128, NKT, 128])
    ada_b = din("ada_b", [1, NMOD * D])
    norms = din("norms", [4, D])
    wg = [din("wg1", [NFT, 128, NKT, 128]), din("wg2", [NFT, 128, NKT, 128])]
    wu = [din("wu1", [NFT, 128, NKT, 128]), din("wu2", [NFT, 128, NKT, 128])]
    wd = [din("wd1", [DFF, D]), din("wd2", [DFF, D])]
    w_in = din("w_in", [INC // 128, 128, NKT, 128])
    w_inv = din("w_inv", [2, 128, NKT, 512])
    pool_w = din("pool_w", [4, 256, 256])
    pool_sc = din("pool_sc", [128, 8])
    w_out = din("w_out", [4, 128, NKT, 512])
    c_ident = din("c_ident", [128, 128])
    c_cosp = din("c_cosp", [2, 128, TPB])
    c_sinp = din("c_sinp", [2, 128, TPB])
    c_coss = din("c_coss", [128, NS])
    c_sins = din("c_sins", [128, NS])
    c_maskp = din("c_maskp", [128, 4, 128])
    c_masks = din("c_masks", [NS, 4, NS])
    c_qdec = din("c_qdec", [128, 8])
    c_kdec = din("c_kdec", [128, 8])
    c_kdecn = din("c_kdecn", [128, 32])
    c_bmask = din("c_bmask", [128, NSB, NS])
    c_rmask = din("c_rmask", [NS, NSB])
    c_invc = din("c_invc", [NB, 128, 4, 16])

    y_p = dout("y_p", [NB * TPB, D])
    y_s = dout("y_s", [NB * NS, D])
    o_poolp = dout("o_poolp", [16, 1024])
    o_retp = dout("o_retp", [4, 256, 256])
    o_pools = dout("o_pools", [NSC, 15, 1024])
    o_rets = dout("o_rets", [NSC, 4, 256, 256])

    mods_d = nc.dram_tensor("mods_d", [1 + NSC, NMOD * D], F32).ap()
    featT_d = nc.dram_tensor("featT_d", [INC, TB], F32).ap()
    S_d = nc.dram_tensor("S_d", [128, 4, 2, 256], F32).ap()
    utail_d = nc.dram_tensor("utail_d", [128, 8, 16], F32).ap()

    with ExitStack() as es:
        b = Bld(nc, es)
        mk = lambda st, name, shape, dt, nreg=1, psum=False: Tile(b, st, name, shape, dt, nreg, psum)
        featreg = b.reg("featT_d")
        sdreg = b.reg("S_d")
        udreg = b.reg("utail_d")

        x = mk(es, "x", [128, NPT + 1, D], F32, nreg=4 * (NPT + 1))
        ident = mk(es, "ident", [128, 128], BF16)
        identf = mk(es, "identf", [128, 128], F32)
        ss = mk(es, "ss", [128, 16], F32)
        rstd = mk(es, "rstd", [128, 16], F32)
        qdec = mk(es, "qdec", [128, 8], F32)
        kdec = mk(es, "kdec", [128, 8], F32)
        psc = mk(es, "psc", [128, 8], F32)
        flg = mk(es, "flg", [128, 1], F32)
        ps = [mk(es, "ps%d" % i, [128, 512], F32, psum=True) for i in range(4)]
        psd = [mk(es, "psd%d" % i, [128, 1024], F32, nreg=2, psum=True) for i in range(2)]
        ps = ps + [View(psd[i][:, 512:1024], [psd[i].r[1]]) for i in range(2)]
        pst = [View(psd[i][:, 0:512].bitcast(BF16), [psd[i].r[0]]) for i in range(2)]

        b.dma("pool", ident[:], c_ident, writes=ident.r)
        b.dma("sp", identf[:], c_ident, writes=identf.r)
        b.dma("sp", qdec[:], c_qdec, writes=qdec.r)
        b.dma("sp", kdec[:], c_kdec, writes=kdec.r)
        b.dma("sp", psc[:], pool_sc, writes=psc.r)
        b.dma("sp", flg[:], flag, writes=flg.r)
        b.op("dve", lambda e: e.memset(ss[:], 1.0), writes=ss.r)

        def tiles_of(grp):
            return list(range(NPT)) if grp == 0 else [NPT]

        def rows_of(t):
            return 128 if t < NPT else NS

        def batched_rstd(tl, sqj, view=None):
            for t in tl:
                R = rows_of(t)
                jv = sqj[0:R, :] if view is None else view[0:R, :]
                b.op("act", lambda e: e.activation(jv, x[0:R, t, :], AF.Square, accum_out=ss[0:R, t:t + 1]),
                     reads=x.r[4 * t:4 * t + 4], writes=sqj.r + ss.r)
            nt = max(tl) + 1
            b.op("dve", lambda e: e.tensor_scalar(rstd[:, 0:nt], ss[:, 0:nt], 1.0 / D, EPS, ALU.mult, ALU.add), reads=ss.r, writes=rstd.r)
            b.op("act", lambda e: e.sqrt(rstd[:, 0:nt], rstd[:, 0:nt]), reads=rstd.r, writes=rstd.r)
            b.op("dve", lambda e: e.reciprocal(rstd[:, 0:nt], rstd[:, 0:nt]), reads=rstd.r, writes=rstd.r)

        csil = mk(es, "csil", [128, NKT, 1 + NSC], BF16)
        modreg = b.reg("mods_d")
        NCH = NMOD * D // 512
        NCH_A = 3 * D // 512
        with ExitStack() as s00:
            ctile = mk(s00, "ctile", [128, NKT, 1 + NSC], F32)
            b.dma("sp", ctile[:], cT.rearrange("(k p) m -> p k m", p=128), writes=ctile.r)
            b.op("act", lambda e: e.activation(csil[:], ctile[:], AF.Silu), reads=ctile.r, writes=csil.r)
        b.barrier()

        def mods_emitters(st, src, nch, cw, col0, psl, nring, src_off=0):
            aw = [mk(st, "aw%d" % i, [128, NKT, cw], BF16) for i in range(nring)]
            mo = [mk(st, "mo%d" % i, [1 + NSC, cw], F32) for i in range(2)]
            bb = [mk(st, "bb%d" % i, [1 + NSC, cw], F32) for i in range(2)]
            M = 1 + NSC

            def load_aw(i):
                b.dma("pool", aw[i % nring][:], src[src_off + i], writes=aw[i % nring].r)

            def emit(i):
                if i == 0:
                    for i2 in range(0, min(nring - 1, nch)):
                        load_aw(i2)
                if i + nring - 1 < nch:
                    load_aw(i + nring - 1)
                a = aw[i % nring]
                p = psl[i % len(psl)]
                bt = bb[i % 2]
                mt = mo[i % 2]
                c0 = col0 + i * cw
                b.dma("sp", bt[:], dap(ada_b, c0, [[0, M], [1, cw]]), writes=bt.r)
                for k in range(NKT):
                    b.op("pe", lambda e, k=k: e.matmul(p[0:M, 0:cw], csil[:, k, :], a[:, k, :], start=(k == 0), stop=(k == NKT - 1)),
                         reads=csil.r + a.r, writes=p.r, inc=(k == NKT - 1))
                b.op("dve", lambda e: e.tensor_tensor(mt[:], p[0:M, 0:cw], bt[:], ALU.add), reads=p.r + bt.r, writes=mt.r)
                b.dma("sp", mods_d[:, c0:c0 + cw], mt[:], reads=mt.r, writes=[modreg])

            return [(lambda i=i: emit(i)) for i in range(nch)]

        with ExitStack() as s0:
            for t in range(NPT):
                b.dma("sp", x[:, t, :], xp_pre[t * 128:(t + 1) * 128, :], writes=x.r[4 * t:4 * t + 4])
            sqj0 = mk(s0, "sqj0", [128, D], BF16)
            batched_rstd(list(range(NPT)), sqj0)
            for em in mods_emitters(s0, ada_w, NCH_A, 512, 0, [ps[0], ps[1]], 3):
                em()
        b.barrier()

        def mod_row_bc(sec, grp, blk):
            if grp == 0:
                return [(dap(mods_d, sec * D, [[0, 128], [1, D]]), slice(0, 128))]
            res = []
            for t in range(4):
                res.append((dap(mods_d, (1 + blk * NSB) * NMOD * D + sec * D, [[NMOD * D, NSB], [1, D]]),
                            slice(t * NSB, (t + 1) * NSB)))
            return res

        def norm_stage(st, hT, gain_idx, sec_sh, sec_sc, blk, sample=True, stats_done=False):
            with ExitStack() as sn:
                groups = (0, 1) if sample else (0,)
                A = [mk(sn, "nA%d" % g_, [128, D], F32) for g_ in groups]
                Bt = [mk(sn, "nB%d" % g_, [128, D], F32) for g_ in groups]
                gt = mk(sn, "ngt", [128, D], F32)
                tmp = mk(sn, "ntmp", [128, D], F32)
                hb = [mk(sn, "nhb%d" % i, [128, D], BF16) for i in range(2)]
                b.dma("sp", gt[:], dap(norms, gain_idx * D, [[0, 128], [1, D]]), writes=gt.r)
                for grp in groups:
                    for (src, psl) in mod_row_bc(sec_sc, grp, blk):
                        b.dma("sp", A[grp][psl, :], src, reads=[modreg], writes=A[grp].r)
                    for (src, psl) in mod_row_bc(sec_sh, grp, blk):
                        b.dma("sp", Bt[grp][psl, :], src, reads=[modreg], writes=Bt[grp].r)
                tl = [t for grp in groups for t in tiles_of(grp)]
                if not stats_done:
                    sqj = mk(sn, "sqj", [128, D], BF16)
                    batched_rstd(tl, sqj)
                for grp in groups:
                    Rg = 128 if grp == 0 else NS
                    b.op("dve", lambda e: e.scalar_tensor_tensor(A[grp][0:Rg, :], A[grp][0:Rg, :], 1.0, gt[0:Rg, :], ALU.add, ALU.mult),
                         reads=A[grp].r + gt.r, writes=A[grp].r)
                cnt = 0
                for grp in groups:
                    for t in tiles_of(grp):
                        R = rows_of(t)
                        h = hb[cnt % 2]
                        cnt += 1
                        b.op("dve", lambda e: e.scalar_tensor_tensor(tmp[0:R, :], x[0:R, t, :], rstd[0:R, t:t + 1], A[grp][0:R, :], ALU.mult, ALU.mult),
                             reads=x.r[4 * t:4 * t + 4] + rstd.r + A[grp].r, writes=tmp.r)
                        b.op("dve", lambda e: e.tensor_tensor(h[0:R, :], tmp[0:R, :], Bt[grp][0:R, :], ALU.add),
                             reads=tmp.r + Bt[grp].r, writes=h.r)
                        for kq in range(4):
                            pt = pst[kq % 2]
                            for kk in range(4):
                                k = kq * 4 + kk
                                b.op("pe", lambda e, k=k, kk=kk: e.transpose(pt[:, kk * 128:kk * 128 + R], h[0:R, k * 128:(k + 1) * 128], ident[0:R, 0:R]),
                                     reads=h.r + ident.r, writes=pt.r, inc=(kk == 3))
                            src = pt[:, 0:512].rearrange("p (a n) -> p a n", a=4)[:, :, 0:R]
                            b.op("act", lambda e, kq=kq, src=src: e.copy(hT[:, kq * 4:(kq + 1) * 4, t * 128:t * 128 + R], src),
                                 reads=pt.r, writes=[hT.r[t]])
            b.barrier()

        def ffn_stage(wi, sec_sh, sec_sc, sec_gt, gain_idx, blk, sample=True, late_spec=None, stats_done=False, next_tiles=None):
            with ExitStack() as sf:
                hT = mk(sf, "hT", [128, NKT, TB], BF16, nreg=NPT + 1)
                norm_stage(sf, hT, gain_idx, sec_sh, sec_sc, blk, sample, stats_done)
                Gp = mk(sf, "Gp", [128, D], F32)
                Gs = mk(sf, "Gs", [128, D], F32)
                NWR = 2
                wgr = [mk(sf, "wgr%d" % i, [128, NKT, 128], BF16) for i in range(NWR)]
                wur = [mk(sf, "wur%d" % i, [128, NKT, 128], BF16) for i in range(NWR)]
                wdr = [mk(sf, "wdr%d" % i, [128, FG, D], BF16, nreg=FG) for i in range(2)]
                aT = [mk(sf, "aT%d" % i, [128, FG, TB], BF16) for i in range(2)]
                sg = [mk(sf, "sg%d" % i, [128, 512], F32) for i in range(2)]
                dtmp = [mk(sf, "dtmp%d" % i, [128, 512], F32) for i in range(1)]
                late = []
                if late_spec is not None:
                    (l_lo, l_n, l_col0) = late_spec
                    late = mods_emitters(sf, ada_wb, l_n, 128, l_col0, [ps[4], ps[5]], 2, src_off=l_lo)
                n_late = len(late)
                for (src, psl) in mod_row_bc(sec_gt, 0, blk):
                    b.dma("sp", Gp[psl, :], src, reads=[modreg], writes=Gp.r)
                for (src, psl) in mod_row_bc(sec_gt, 1, blk):
                    b.dma("sp", Gs[psl, :], src, reads=[modreg], writes=Gs.r)
                b.op("dve", lambda e: e.tensor_scalar_mul(Gp[:], Gp[:], 0.5), reads=Gp.r, writes=Gp.r)
                b.op("dve", lambda e: e.tensor_scalar_mul(Gs[0:NS, :], Gs[0:NS, :], 0.5), reads=Gs.r, writes=Gs.r)
                wdv = wd[wi].rearrange("(f p) n -> p f n", p=128)

                def load_up(f):
                    b.dma("pool", wgr[f % NWR][:], wg[wi][f], writes=wgr[f % NWR].r)
                    b.dma("pool", wur[f % NWR][:], wu[wi][f], writes=wur[f % NWR].r)

                def load_down(j):
                    b.dma("pool", wdr[j % 2][:], wdv[:, j * FG:(j + 1) * FG, :], writes=wdr[j % 2].r)

                tcs = [(0, 512), (512, 512), (1024, NS)] if sample else [(0, 512), (512, 512)]
                state = {"pi": 0, "sgi": 0, "dp": 0, "dt": 0}

                def up(j):
                    at = aT[j % 2]
                    for fi in range(FG):
                        f = j * FG + fi
                        want = (n_late * (f + 1)) // NFT
                        while late and n_late - len(late) < want:
                            late.pop(0)()
                        if f + NWR - 1 < NFT:
                            load_up(f + NWR - 1)
                        g_w = wgr[f % NWR]
                        u_w = wur[f % NWR]
                        for (c0, cn) in tcs:
                            pg = ps[state["pi"] % 4]
                            pu = ps[(state["pi"] + 1) % 4]
                            state["pi"] += 2
                            for k in range(NKT):
                                b.op("pe", lambda e, k=k: e.matmul(pg[:, 0:cn], g_w[:, k, :], hT[:, k, c0:c0 + cn], start=(k == 0), stop=(k == NKT - 1)),
                                     reads=g_w.r + hT.r, writes=pg.r, inc=(k == NKT - 1))
                            for k in range(NKT):
                                b.op("pe", lambda e, k=k: e.matmul(pu[:, 0:cn], u_w[:, k, :], hT[:, k, c0:c0 + cn], start=(k == 0), stop=(k == NKT - 1)),
                                     reads=u_w.r + hT.r, writes=pu.r, inc=(k == NKT - 1))
                            s_ = sg[state["sgi"] % 2]
                            state["sgi"] += 1
                            b.op("act", lambda e: e.activation(s_[:, 0:cn], pg[:, 0:cn], AF.Silu), reads=pg.r, writes=s_.r)
                            b.op("dve", lambda e: e.tensor_tensor(at[:, fi, c0:c0 + cn], s_[:, 0:cn], pu[:, 0:cn], ALU.mult),
                                 reads=s_.r + pu.r, writes=at.r)

                def down_sample(j):
                    at = aT[j % 2]
                    w = wdr[j % 2]
                    for n4 in (range(4) if sample else ()):
                        p = ps[4 + state["dp"] % 2]
                        state["dp"] += 1
                        for fi in range(FG):
                            b.op("pe", lambda e, fi=fi: e.matmul(p[0:NS, :], at[:, fi, TPB:TB], w[:, fi, n4 * 512:(n4 + 1) * 512], start=(fi == 0), stop=(fi == FG - 1)),
                                 reads=at.r + [w.r[fi]], writes=p.r, inc=(fi == FG - 1))
                        d_ = dtmp[0]
                        state["dt"] += 1
                        b.op("dve", lambda e: e.tensor_tensor(d_[0:NS, :], p[0:NS, :], Gs[0:NS, n4 * 512:(n4 + 1) * 512], ALU.mult),
                             reads=p.r + Gs.r, writes=d_.r)
                        b.op("pool", lambda e: e.tensor_tensor(x[0:NS, NPT, n4 * 512:(n4 + 1) * 512], x[0:NS, NPT, n4 * 512:(n4 + 1) * 512], d_[0:NS, :], ALU.add),
                             reads=d_.r + [x.r[4 * NPT + n4]], writes=[x.r[4 * NPT + n4]])
                    for fi in range(FG):
                        b.op("dve", lambda e, fi=fi: e.tensor_tensor(w[:, fi, :], w[:, fi, :], Gp[:], ALU.mult),
                             reads=[w.r[fi]] + Gp.r, writes=[w.r[fi]])

                def down_prompt(j):
                    at = aT[j % 2]
                    w = wdr[j % 2]
                    for t in range(NPT):
                        for n2 in range(2):
                            p = psd[state["dp"] % 2]
                            state["dp"] += 1
                            for hf in range(2):
                                n4 = 2 * n2 + hf
                                for fi in range(FG):
                                    b.op("pe", lambda e, fi=fi: e.matmul(p[:, hf * 512:(hf + 1) * 512], at[:, fi, t * 128:(t + 1) * 128], w[:, fi, n4 * 512:(n4 + 1) * 512], start=(fi == 0), stop=(fi == FG - 1)),
                                         reads=at.r + [w.r[fi]], writes=[p.r[hf]], inc=(fi == FG - 1))
                            b.op("dve", lambda e: e.tensor_tensor(x[:, t, n2 * 1024:(n2 + 1) * 1024], x[:, t, n2 * 1024:(n2 + 1) * 1024], p[:, :], ALU.add),
                                 reads=p.r + x.r[4 * t + 2 * n2:4 * t + 2 * n2 + 2], writes=x.r[4 * t + 2 * n2:4 * t + 2 * n2 + 2])

                for f in range(NWR - 1):
                    load_up(f)
                load_down(0)
                up(0)
                down_sample(0)
                for j in range(NG):
                    if j + 1 < NG:
                        up(j + 1)
                        load_down(j + 1)
                    down_prompt(j)
                    if j + 1 < NG:
                        down_sample(j + 1)
                if next_tiles is not None:
                    jv = aT[(NG - 2) % 2][:, 0:2, :].rearrange("p a n -> p (a n)")[:, 0:D]
                    batched_rstd(next_tiles, aT[(NG - 2) % 2], jv)
            b.barrier()


        def rotary_pair(xa, xb, cosT, sinT, rt, tcs):
            for ci, (c0, cn) in enumerate(tcs):
                t1, t2, t3, t4 = rt
                b.op("dve", lambda e: e.tensor_tensor(t1[:, 0:cn], xa[:, c0:c0 + cn], cosT[:, c0:c0 + cn], ALU.mult), reads=[xa.r[ci]] + cosT.r, writes=t1.r)
                b.op("dve", lambda e: e.tensor_tensor(t2[:, 0:cn], xb[:, c0:c0 + cn], sinT[:, c0:c0 + cn], ALU.mult), reads=[xb.r[ci]] + sinT.r, writes=t2.r)
                b.op("pool", lambda e: e.tensor_tensor(t3[:, 0:cn], xb[:, c0:c0 + cn], cosT[:, c0:c0 + cn], ALU.mult), reads=[xb.r[ci]] + cosT.r, writes=t3.r)
                b.op("pool", lambda e: e.tensor_tensor(t4[:, 0:cn], xa[:, c0:c0 + cn], sinT[:, c0:c0 + cn], ALU.mult), reads=[xa.r[ci]] + sinT.r, writes=t4.r)
                b.op("dve", lambda e: e.tensor_tensor(xa[:, c0:c0 + cn], t1[:, 0:cn], t2[:, 0:cn], ALU.subtract), reads=t1.r + t2.r, writes=[xa.r[ci]])
                b.op("pool", lambda e: e.tensor_tensor(xb[:, c0:c0 + cn], t3[:, 0:cn], t4[:, 0:cn], ALU.add), reads=t3.r + t4.r, writes=[xb.r[ci]])

        def mixer_lite():
            with ExitStack() as sm:
                v = mk(sm, "v", [128, NPT + 1, 1024], BF16, nreg=NPT + 1)
                tcs = [(0, 512), (512, 512)]
                with ExitStack() as sa:
                    hT = mk(sa, "hT", [128, NKT, TB], BF16, nreg=NPT + 1)
                    norm_stage(sa, hT, 1, 3, 4, 0, sample=False, stats_done=True)
                    load_x(False)
                    NWI = 3
                    wi = [mk(sa, "wi%d" % i, [128, NKT, 128], BF16) for i in range(NWI)]
                    wv = [mk(sa, "wv%d" % i, [128, NKT, 512], BF16) for i in range(2)]
                    stg = [mk(sa, "stg%d" % i, [128, TB], F32, nreg=3) for i in range(4)]
                    cosT = mk(sa, "cosT", [128, TPB], F32)
                    sinT = mk(sa, "sinT", [128, TPB], F32)
                    rt = [mk(sa, "rt%d" % i, [128, 512], F32) for i in range(4)]
                    b.dma("sp", cosT[:], c_cosp[0], writes=cosT.r)
                    b.dma("sp", sinT[:], c_sinp[0], writes=sinT.r)
                    cts = list(range(0, 8)) + list(range(16, 24))

                    def load_wi(i):
                        b.dma("pool", wi[i % NWI][:], w_in[cts[i]], writes=wi[i % NWI].r)

                    for vc in range(2):
                        b.dma("pool", wv[vc][:], w_inv[vc], writes=wv[vc].r)
                    for i in range(NWI - 1):
                        load_wi(i)
                    pi = 0
                    for i, ct in enumerate(cts):
                        if i + NWI - 1 < len(cts):
                            load_wi(i + NWI - 1)
                        w_ = wi[i % NWI]
                        st_ = stg[i % 4]
                        for ci, (c0, cn) in enumerate(tcs):
                            p = ps[pi % 4]
                            pi += 1
                            for k in range(NKT):
                                b.op("pe", lambda e, k=k: e.matmul(p[:, 0:cn], w_[:, k, :], hT[:, k, c0:c0 + cn], start=(k == 0), stop=(k == NKT - 1)),
                                     reads=w_.r + hT.r, writes=p.r, inc=(k == NKT - 1))
                            b.op("act", lambda e: e.copy(st_[:, c0:c0 + cn], p[:, 0:cn]), reads=p.r, writes=[st_.r[ci]])
                        if ct < 16:
                            b.dma("sp", featT_d[ct * 128:(ct + 1) * 128, 0:TPB], st_[:, 0:TPB], reads=st_.r, writes=[featreg])
                        elif ct % 2 == 1:
                            sa_p = stg[(i - 1) % 4]
                            rotary_pair(sa_p, st_, cosT, sinT, rt, tcs)
                            b.dma("sp", featT_d[(ct - 1) * 128:ct * 128, 0:TPB], sa_p[:, 0:TPB], reads=sa_p.r, writes=[featreg])
                            b.dma("sp", featT_d[ct * 128:(ct + 1) * 128, 0:TPB], st_[:, 0:TPB], reads=st_.r, writes=[featreg])
                    for vc in range(2):
                        for t in range(NPT):
                            p = ps[4 + (pi % 2)]
                            pi += 1
                            for k in range(NKT):
                                b.op("pe", lambda e, k=k: e.matmul(p[:, :], hT[:, k, t * 128:(t + 1) * 128], wv[vc][:, k, :], start=(k == 0), stop=(k == NKT - 1)),
                                     reads=wv[vc].r + hT.r, writes=p.r, inc=(k == NKT - 1))
                            if t % 2 == 0:
                                b.op("act", lambda e: e.copy(v[:, t, vc * 512:(vc + 1) * 512], p[:, :]), reads=p.r, writes=[v.r[t]])
                            else:
                                b.op("dve", lambda e: e.tensor_copy(v[:, t, vc * 512:(vc + 1) * 512], p[:, :]), reads=p.r, writes=[v.r[t]])
                b.barrier()
                with ExitStack() as sr:
                    kTs = [mk(sr, "kT%d" % i, [128, 2, TPB], BF16) for i in range(2)]

                    def load_k(h):
                        row0 = 2048 + h * 256
                        b.dma("pool", kTs[h % 2][:], featT_d[row0:row0 + 256, 0:TPB].rearrange("(j p) n -> p j n", p=128),
                              reads=[featreg], writes=kTs[h % 2].r)

                    kd = [mk(sr, "kd%d" % i, [128, 256], BF16) for i in range(2)]
                    S = mk(sr, "S", [128, 4, 2, 256], F32, nreg=4)
                    utail = mk(sr, "utail", [128, 8, 16], F32)
                    for ct in range(8):
                        b.dma("sp", utail[:, ct, :], featT_d[ct * 128:(ct + 1) * 128, TPB - 16:TPB], reads=[featreg], writes=utail.r)
                    kdn = mk(sr, "kdn", [128, 32], F32)
                    b.dma("sp", kdn[:], c_kdecn, writes=kdn.r)
                    late = []
                    cnt = {"rc": 0, "c": 0, "pt": 0, "it": 0}
                    load_k(0)
                    for h in range(4):
                        if h + 1 < 4:
                            load_k(h + 1)
                        kT = kTs[h % 2]
                        pD = [ps[2 * (h % 2)], ps[2 * (h % 2) + 1]]
                        pend = []
                        for n in range(NPT):
                            c0 = n * 128
                            i2 = cnt["c"] % 2
                            cnt["c"] += 1
                            pt_ = pst[cnt["pt"] % 2]
                            cnt["pt"] += 1
                            for j in range(2):
                                b.op("pe", lambda e, j=j: e.transpose(pt_[:, j * 128:(j + 1) * 128], kT[:, j, c0:c0 + 128], ident[:, :]),
                                     reads=kT.r + ident.r, writes=pt_.r, inc=(j == 1))
                            kd_ = kd[i2]
                            b.op("dve", lambda e: e.tensor_scalar_mul(kd_[:, :], pt_[:, 0:256], kdn[:, n * 4 + h:n * 4 + h + 1]), reads=pt_.r + kdn.r, writes=kd_.r)

                            def acc(kd_=kd_, n=n, h=h, pD=pD):
                                for j in range(2):
                                    b.op("pe", lambda e, j=j: e.matmul(pD[j][:, 0:256], kd_[:, j * 128:(j + 1) * 128], v[:, n, h * 256:(h + 1) * 256], start=(n == 0), stop=(n == NPT - 1)),
                                         reads=kd_.r + [v.r[n]], writes=pD[j].r, inc=True)

                            pend.append(acc)
                            while len(pend) > 1:
                                pend.pop(0)()
                        while pend:
                            pend.pop(0)()
                        for j in range(2):
                            b.op("dve", lambda e, j=j: e.tensor_scalar_mul(S[:, h, j, :], pD[j][:, 0:256], flg[:, 0:1]), reads=pD[j].r + flg.r, writes=[S.r[h]])
                    while late:
                        late.pop(0)()
                    b.op("dve", lambda e: e.tensor_scalar_mul(utail[:], utail[:], flg[:, 0:1]), reads=utail.r + flg.r, writes=utail.r)
                    b.dma("sp", S_d, S[:], reads=S.r, writes=[sdreg])
                    b.dma("sp", utail_d, utail[:], reads=utail.r, writes=[udreg])
            b.barrier()

        def mixer_stage(blk):
            with ExitStack() as sm:
                v = mk(sm, "v", [128, NPT + 1, 1024], BF16, nreg=NPT + 1)
                tcs = [(0, 512), (512, 512), (1024, NS)]
                with ExitStack() as sa:
                    hT = mk(sa, "hT", [128, NKT, TB], BF16, nreg=NPT + 1)
                    norm_stage(sa, hT, 1, 3, 4, blk, stats_done=True)
                    NWI = 3
                    wi = [mk(sa, "wi%d" % i, [128, NKT, 128], BF16) for i in range(NWI)]
                    wv = [mk(sa, "wv%d" % i, [128, NKT, 512], BF16) for i in range(2)]
                    stg = [mk(sa, "stg%d" % i, [128, TB], F32, nreg=3) for i in range(4)]
                    cosT = mk(sa, "cosT", [128, TB], F32)
                    sinT = mk(sa, "sinT", [128, TB], F32)
                    rt = [mk(sa, "rt%d" % i, [128, 512], F32) for i in range(4)]
                    b.dma("sp", cosT[:, 0:TPB], c_cosp[1], writes=cosT.r)
                    b.dma("sp", cosT[:, TPB:TB], c_coss, writes=cosT.r)
                    b.dma("sp", sinT[:, 0:TPB], c_sinp[1], writes=sinT.r)
                    b.dma("sp", sinT[:, TPB:TB], c_sins, writes=sinT.r)
                    cts = list(range(0, 24)) + list(range(32, 40))

                    def load_wi(i):
                        ct = cts[i]
                        b.dma("pool", wi[i % NWI][:], w_in[ct], writes=wi[i % NWI].r)

                    for vc in range(2):
                        b.dma("pool", wv[vc][:], w_inv[vc], writes=wv[vc].r)
                    for i in range(NWI - 1):
                        load_wi(i)
                    pi = 0
                    for i, ct in enumerate(cts):
                        if i + NWI - 1 < len(cts):
                            load_wi(i + NWI - 1)
                        w_ = wi[i % NWI]
                        st_ = stg[i % 4]
                        for ci, (c0, cn) in enumerate(tcs):
                            p = ps[pi % 4]
                            pi += 1
                            for k in range(NKT):
                                b.op("pe", lambda e, k=k: e.matmul(p[:, 0:cn], w_[:, k, :], hT[:, k, c0:c0 + cn], start=(k == 0), stop=(k == NKT - 1)),
                                     reads=w_.r + hT.r, writes=p.r, inc=(k == NKT - 1))
                            if ct >= 32:
                                b.op("act", lambda e: e.activation(st_[:, c0:c0 + cn], p[:, 0:cn], AF.Silu), reads=p.r, writes=[st_.r[ci]])
                            else:
                                b.op("act", lambda e: e.copy(st_[:, c0:c0 + cn], p[:, 0:cn]), reads=p.r, writes=[st_.r[ci]])
                        if ct < 8 or ct >= 32:
                            b.dma("sp", featT_d[ct * 128:(ct + 1) * 128, :], st_[:], reads=st_.r, writes=[featreg])
                        elif ct % 2 == 1:
                            sa_p = stg[(i - 1) % 4]
                            rotary_pair(sa_p, st_, cosT, sinT, rt, tcs)
                            b.dma("sp", featT_d[(ct - 1) * 128:ct * 128, :], sa_p[:], reads=sa_p.r, writes=[featreg])
                            b.dma("sp", featT_d[ct * 128:(ct + 1) * 128, :], st_[:], reads=st_.r, writes=[featreg])
                    for vc in range(2):
                        for t in range(NPT + 1):
                            R = rows_of(t)
                            p = ps[4 + (pi % 2)]
                            pi += 1
                            for k in range(NKT):
                                b.op("pe", lambda e, k=k: e.matmul(p[0:R, :], hT[:, k, t * 128:t * 128 + R], wv[vc][:, k, :], start=(k == 0), stop=(k == NKT - 1)),
                                     reads=wv[vc].r + hT.r, writes=p.r, inc=(k == NKT - 1))
                            if t % 2 == 0:
                                b.op("act", lambda e: e.copy(v[0:R, t, vc * 512:(vc + 1) * 512], p[0:R, :]), reads=p.r, writes=[v.r[t]])
                            else:
                                b.op("dve", lambda e: e.tensor_copy(v[0:R, t, vc * 512:(vc + 1) * 512], p[0:R, :]), reads=p.r, writes=[v.r[t]])
                b.barrier()
                mixT = mk(sm, "mixT", [128, NKT, TB], BF16)
                with ExitStack() as sp_:
                    pw = mk(sp_, "pw", [128, 4, 2, 256], BF16)
                    invc = mk(sp_, "invc", [128, 4, 16], F32)
                    ue_l = [mk(sp_, "ue%d" % i, [128, 2, 16 + TPB], F32) for i in range(2)]
                    wa = mk(sp_, "wa", [128, 2, 16 + TPB], F32)
                    wb = mk(sp_, "wb", [128, 2, 16 + TPB], F32)
                    mT_l = [mk(sp_, "mT%d" % i, [128, 2, TB], BF16) for i in range(2)]
                    ues = mk(sp_, "ues", [128, 2, 19, NSB], F32)
                    sa_ = mk(sp_, "sa_", [128, 2, 19, NSB], F32)
                    sb_ = mk(sp_, "sb_", [128, 2, 19, NSB], F32)
                    t16 = mk(sp_, "t16", [128, 2, 16], F32)
                    sptm = [mk(sp_, "sptm%d" % i, [120, 1024], F32) for i in range(2)]
                    ustm = mk(sp_, "ustm", [NS, 1024], F32)
                    utail = mk(sp_, "utail", [128, 8, 16], F32, nreg=8)
                    b.dma("sp", utail[:], utail_d, reads=[udreg], writes=utail.r)
                    b.dma("pool", pw[:], pool_w.rearrange("g (j p) d -> p g j d", p=128), writes=pw.r)
                    b.dma("sp", invc[:], c_invc[blk], writes=invc.r)
                    for i in range(2):
                        b.dma("sp", sptm[i][:], spool[blk * NSB + i * 8: blk * NSB + (i + 1) * 8].rearrange("b s c -> (b s) c"), writes=sptm[i].r)
                    b.dma("sp", o_pools[blk * NSB:(blk + 1) * NSB, 0:11, :], spool[blk * NSB:(blk + 1) * NSB, 4:15, :])
                    pi = 0
                    for g in range(4):
                        w = 2 ** (g + 1)
                        ue, mT = ue_l[g % 2], mT_l[g % 2]
                        for j in range(2):
                            ct = 2 * g + j
                            b.op("act", lambda e: e.copy(ue[:, j, 0:16], utail[:, ct, :]), reads=[utail.r[ct]], writes=ue.r)
                            b.dma("sp", ue[:, j, 16:16 + TPB], featT_d[ct * 128:(ct + 1) * 128, 0:TPB], reads=[featreg], writes=ue.r)
                            b.dma("sp", ues[:, j, 15:19, :], featT_d[ct * 128:(ct + 1) * 128, TPB:TB].rearrange("p (t b) -> p t b", t=4), reads=[featreg], writes=ues.r)
                            for i in range(2):
                                pt_ = ps[pi % 4]
                                pi += 1
                                b.op("pe", lambda e: e.transpose(pt_[:, 0:120], sptm[i][:, ct * 128:(ct + 1) * 128], identf[0:120, 0:120]),
                                     reads=sptm[i].r + identf.r, writes=pt_.r)
                                dst = ues[:, j, 0:15, i * 8:(i + 1) * 8].rearrange("p s b -> p b s")
                                b.op("act", lambda e: e.copy(dst, pt_[:, 0:120].rearrange("p (b s) -> p b s", b=8)), reads=pt_.r, writes=ues.r)
                        L = 16 + TPB
                        src = ue
                        bufs = [wa, wb]
                        bi_ = 0
                        sh = 1
                        while sh < w:
                            cur = bufs[bi_]
                            bi_ ^= 1
                            b.op("dve", lambda e: e.tensor_tensor(cur[:, :, sh:L], src[:, :, sh:L], src[:, :, 0:L - sh], ALU.add),
                                 reads=src.r, writes=cur.r)
                            src = cur
                            sh *= 2
                        wsum = src
                        b.op("dve", lambda e: e.scalar_tensor_tensor(mT[:, :, 0:TPB], wsum[:, :, 16:L], 1.0 / w, ue[:, :, 16:L], ALU.mult, ALU.subtract),
                             reads=wsum.r + ue.r, writes=mT.r)
                        for j in range(2):
                            b.op("dve", lambda e: e.tensor_tensor(t16[:, j, :], wsum[:, j, 16:32], invc[:, g, :], ALU.mult),
                                 reads=wsum.r + invc.r, writes=t16.r)
                        b.op("dve", lambda e: e.tensor_tensor(mT[:, :, 0:16], t16[:], ue[:, :, 16:32], ALU.subtract),
                             reads=t16.r + ue.r, writes=mT.r)
                        src, cur = ues, sa_
                        sh = 1
                        while sh < w:
                            b.op("pool", lambda e: e.tensor_tensor(cur[:, :, sh:19, :], src[:, :, sh:19, :], src[:, :, 0:19 - sh, :], ALU.add),
                                 reads=src.r, writes=cur.r)
                            nxt = sb_ if cur is sa_ else sa_
                            src, cur = cur, nxt
                            sh *= 2
                        wsum_s = src
                        b.op("dve", lambda e: e.scalar_tensor_tensor(mT[:, :, TPB:TB].rearrange("p j (t b) -> p j t b", t=4), wsum_s[:, :, 15:19, :], 1.0 / w, ues[:, :, 15:19, :], ALU.mult, ALU.subtract),
                             reads=wsum_s.r + ues.r, writes=mT.r)
                        for j in range(2):
                            ct = 2 * g + j
                            b.op("act", lambda e: e.copy(utail[:, ct, :], ue[:, j, TPB:TPB + 16]), reads=ue.r, writes=[utail.r[ct]])
                            pt_ = ps[pi % 4]
                            pi += 1
                            b.op("pe", lambda e: e.transpose(pt_[0:NS, 0:128], ues[:, j, 15:19, :].rearrange("p t b -> p (t b)"), identf[:, :]),
                                 reads=ues.r + identf.r, writes=pt_.r)
                            b.op("act", lambda e: e.copy(ustm[:, ct * 128:(ct + 1) * 128], pt_[0:NS, 0:128]), reads=pt_.r, writes=ustm.r)
                        for dt in range(2):
                            for (c0, cn) in tcs:
                                p = ps[4]
                                pi += 1
                                for j in range(2):
                                    b.op("pe", lambda e, j=j: e.matmul(p[:, 0:cn], pw[:, g, j, dt * 128:(dt + 1) * 128], mT[:, j, c0:c0 + cn], start=(j == 0), stop=(j == 1)),
                                         reads=pw.r + mT.r, writes=p.r, inc=(j == 1))
                                b.op("dve", lambda e: e.tensor_scalar_mul(mixT[:, 2 * g + dt, c0:c0 + cn], p[:, 0:cn], psc[:, 2 * g + dt:2 * g + dt + 1]),
                                     reads=p.r + psc.r, writes=mixT.r)
                    for t in range(4):
                        b.dma("sp", o_pools[blk * NSB:(blk + 1) * NSB, 11 + t, :], ustm[t * NSB:(t + 1) * NSB, :], reads=ustm.r)
                    if blk == NB - 1:
                        pl = mk(sp_, "pl", [16, 1024], F32)
                        for ct in range(8):
                            pt_ = ps[pi % 4]
                            pi += 1
                            b.op("pe", lambda e: e.transpose(pt_[0:16, 0:128], utail[:, ct, :], identf[:, :]), reads=[utail.r[ct]] + identf.r, writes=pt_.r)
                            b.op("act", lambda e: e.copy(pl[:, ct * 128:(ct + 1) * 128], pt_[0:16, 0:128]), reads=pt_.r, writes=pl.r)
                        b.dma("sp", o_poolp, pl[:], reads=pl.r)
                b.barrier()
                with ExitStack() as sr:
                    maskp = mk(sr, "maskp", [128, 4, 128], F32)
                    masks = mk(sr, "masks", [NS, 4, NS], F32)
                    bmask = mk(sr, "bmask", [128, NSB, NS], BF16)
                    rmask = mk(sr, "rmask", [NS, NSB], F32)
                    qTs = [mk(sr, "qT%d" % i, [128, 2, TB], BF16) for i in range(2)]
                    kTs = [mk(sr, "kT%d" % i, [128, 2, TB], BF16) for i in range(2)]

                    def load_qk(h):
                        for (dst, row0) in ((qTs[h % 2], 1024 + h * 256), (kTs[h % 2], 2048 + h * 256)):
                            b.dma("pool", dst[:], featT_d[row0:row0 + 256, :].rearrange("(j p) n -> p j n", p=128),
                                  reads=[featreg], writes=dst.r)

                    gsc = [mk(sr, "gsc%d" % i, [128, 2, 128], F32) for i in range(2)]
                    sT = [mk(sr, "sT%d" % i, [128, 128], BF16) for i in range(2)]
                    kd = [mk(sr, "kd%d" % i, [128, 256], BF16) for i in range(2)]
                    tmpc = [mk(sr, "tmpc%d" % i, [128, 256], F32) for i in range(2)]
                    o_ = [mk(sr, "o%d" % i, [128, 256], F32) for i in range(2)]
                    on = [mk(sr, "on%d" % i, [128, 256], BF16) for i in range(2)]
                    junk = mk(sr, "rjunk", [128, 256], BF16)
                    sso = mk(sr, "sso", [128, 2], F32)
                    qm = mk(sr, "qm", [128, 2, NSB, NS], BF16, nreg=2 * NSB)
                    kdmb = [mk(sr, "kdmb%d" % i, [NS, 256], BF16) for i in range(4)]
                    s0f = [mk(sr, "s0f%d" % i, [128, 2, 256], F32) for i in range(4)]
                    s0b = [mk(sr, "s0b%d" % i, [128, 2, 256], BF16) for i in range(4)]
                    sno = [mk(sr, "sno%d" % i, [128, 2, 256], F32) for i in range(2)]
                    S = mk(sr, "S", [128, 4, 2, 256], F32, nreg=4)
                    Sb = mk(sr, "Sb", [128, 4, 2, 256], BF16, nreg=4)
                    b.dma("sp", S[:], S_d, reads=[sdreg], writes=S.r)
                    for h in range(4):
                        b.op("act", lambda e: e.copy(Sb[:, h, :, :], S[:, h, :, :]), reads=[S.r[h]], writes=[Sb.r[h]])
                    b.dma("sp", maskp[:], c_maskp, writes=maskp.r)
                    b.dma("sp", masks[:], c_masks, writes=masks.r)
                    b.dma("pool", bmask[:], c_bmask, writes=bmask.r)
                    b.dma("sp", rmask[:], c_rmask, writes=rmask.r)
                    cnt = {"rc": 0, "t": 0, "c": 0, "pa": 0, "pt": 0, "s0": 0, "sn": 0}
                    pending = []
                    late_r = mods_emitters(sr, ada_wb, 3 * D // 128, 128, 6 * D, [ps[3]], 3, src_off=3 * D // 128)
                    for h in range(4):
                        gam = GAM[h]
                        if h == 0:
                            load_qk(0)
                        if h + 1 < 4:
                            load_qk(h + 1)
                        qT, kT = qTs[h % 2], kTs[h % 2]
                        for n in range(NPT + 1):
                            R = rows_of(n)
                            c0 = n * 128
                            hh = h if n < NPT else 4 + h
                            i2 = cnt["c"] % 2
                            cnt["c"] += 1
                            g_ = gsc[i2]
                            b.dma("sp", g_[:, :, 0:R], featT_d[4096 + h * 256:4096 + (h + 1) * 256, c0:c0 + R].rearrange("(j p) n -> p j n", p=128),
                                  reads=[featreg], writes=g_.r)
                            pA = ps[cnt["pa"] % 3]
                            cnt["pa"] += 1
                            for j in range(2):
                                b.op("pe", lambda e, j=j: e.matmul(pA[0:R, 0:R], kT[:, j, c0:c0 + R], qT[:, j, c0:c0 + R], start=(j == 0), stop=(j == 1)),
                                     reads=kT.r + qT.r, writes=pA.r, inc=(j == 1))
                            s_ = sT[i2]
                            mk_ = maskp[:, h, :] if n < NPT else masks[:, h, :]
                            b.op("dve", lambda e: e.tensor_tensor(s_[0:R, 0:R], pA[0:R, 0:R], mk_, ALU.mult), reads=pA.r + maskp.r + masks.r, writes=s_.r)
                            pt_ = pst[cnt["pt"] % 2]
                            cnt["pt"] += 1
                            for j in range(2):
                                b.op("pe", lambda e, j=j: e.transpose(pt_[0:R, j * 128:(j + 1) * 128], kT[:, j, c0:c0 + R], ident[:, :]),
                                     reads=kT.r + ident.r, writes=pt_.r, inc=(j == 1))
                            kd_ = kd[i2]
                            b.op("act", lambda e: e.mul(kd_[0:R, :], pt_[0:R, 0:256], kdec[0:R, hh:hh + 1]), reads=pt_.r + kdec.r, writes=kd_.r)
                            pB = ps[cnt["pa"] % 3]
                            cnt["pa"] += 1
                            b.op("pe", lambda e: e.matmul(pB[0:R, 0:256], s_[0:R, 0:R], v[0:R, n, h * 256:(h + 1) * 256], start=True, stop=True),
                                 reads=s_.r + [v.r[n]], writes=pB.r)
                            pC = ps[cnt["pa"] % 3]
                            cnt["pa"] += 1
                            if n < NPT:
                                for j in range(2):
                                    b.op("pe", lambda e, j=j: e.matmul(pC[0:R, 0:256], qT[:, j, c0:c0 + R], Sb[:, h, j, :], start=(j == 0), stop=(j == 1)),
                                         reads=qT.r + [Sb.r[h]], writes=pC.r, inc=(j == 1))
                            else:
                                for j in range(2):
                                    eng = "dve" if j == 0 else "pool"
                                    for bq in range(NSB):
                                        b.op(eng, lambda e: e.tensor_tensor(qm[:, j, bq, :], qT[:, j, c0:c0 + R], bmask[:, bq, :], ALU.mult),
                                             reads=qT.r + bmask.r, writes=[qm.r[j * NSB + bq]])
                                NS0 = len(s0f)

                                def load_s0(bi):
                                    i3 = bi % NS0
                                    src = sret[blk * NSB + bi, h].rearrange("(j p) v -> p j v", p=128)
                                    b.dma("sp", s0f[i3][:], src, writes=s0f[i3].r)
                                    b.dma("pool", s0b[i3][:], src, writes=s0b[i3].r)

                                for bi in range(NS0 - 1):
                                    load_s0(bi)
                                for bi in range(NSB):
                                    if bi + NS0 - 1 < NSB:
                                        load_s0(bi + NS0 - 1)
                                    i3 = bi % NS0
                                    for j in range(2):
                                        b.op("pe", lambda e, j=j: e.matmul(pC[0:R, 0:256], qm[:, j, bi, :], s0b[i3][:, j, :], start=(bi == 0 and j == 0), stop=(bi == NSB - 1 and j == 1)),
                                             reads=[qm.r[j * NSB + bi]] + s0b[i3].r, writes=pC.r, inc=(j == 1))
                                    km_ = kdmb[bi % len(kdmb)]
                                    b.op("dve", lambda e: e.tensor_scalar_mul(km_[:, :], kd_[0:NS, :], rmask[:, bi:bi + 1]),
                                         reads=kd_.r + rmask.r, writes=km_.r)
                                    sn_ = sno[bi % len(sno)]
                                    for j in range(2):
                                        pD = ps[4 + j]
                                        b.op("pe", lambda e: e.matmul(pD[:, 0:256], km_[:, j * 128:(j + 1) * 128], v[0:NS, n, h * 256:(h + 1) * 256], start=True, stop=True),
                                             reads=km_.r + [v.r[n]], writes=pD.r)
                                        b.op("dve", lambda e: e.scalar_tensor_tensor(sn_[:, j, :], s0f[i3][:, j, :], gam ** 4, pD[:, 0:256], ALU.mult, ALU.add),
                                             reads=s0f[i3].r + pD.r, writes=sn_.r)
                                    b.dma("sp", o_rets[blk * NSB + bi, h].rearrange("(j p) v -> p j v", p=128), sn_[:], reads=sn_.r)
                            tc_ = tmpc[i2]
                            b.op("act", lambda e: e.mul(tc_[0:R, :], pC[0:R, 0:256], qdec[0:R, hh:hh + 1]), reads=pC.r + qdec.r, writes=tc_.r)
                            oo = o_[i2]
                            b.op("dve", lambda e: e.tensor_tensor(oo[0:R, :], pB[0:R, 0:256], tc_[0:R, :], ALU.add), reads=pB.r + tc_.r, writes=oo.r)
                            if n < NPT:
                                pD = ps[4 + cnt["c"] % 2]
                                for j in range(2):
                                    b.op("pe", lambda e, j=j: e.matmul(pD[:, j * 256:(j + 1) * 256], kd_[:, j * 128:(j + 1) * 128], v[:, n, h * 256:(h + 1) * 256], start=True, stop=True),
                                         reads=kd_.r + [v.r[n]], writes=pD.r, inc=(j == 1))
                                b.op("dve", lambda e: e.scalar_tensor_tensor(S[:, h, :, :], S[:, h, :, :], gam ** 128, pD[:, :].rearrange("p (j v) -> p j v", j=2), ALU.mult, ALU.add),
                                     reads=[S.r[h]] + pD.r, writes=[S.r[h]])
                                b.op("act", lambda e: e.copy(Sb[:, h, :, :], S[:, h, :, :]), reads=[S.r[h]], writes=[Sb.r[h]])
                                if n == NPT - 1 and blk == NB - 1:
                                    b.dma("sp", o_retp[h].rearrange("(j p) v -> p j v", p=128), S[:, h, :, :], reads=[S.r[h]])
                            b.op("act", lambda e: e.activation(junk[0:R, :], oo[0:R, :], AF.Square, accum_out=sso[0:R, 0:1]), reads=oo.r, writes=junk.r + sso.r)
                            b.op("dve", lambda e: e.tensor_scalar(sso[0:R, 1:2], sso[0:R, 0:1], 1.0 / 256, EPS, ALU.mult, ALU.add), reads=sso.r, writes=sso.r)
                            b.op("act", lambda e: e.sqrt(sso[0:R, 1:2], sso[0:R, 1:2]), reads=sso.r, writes=sso.r)
                            b.op("dve", lambda e: e.reciprocal(sso[0:R, 1:2], sso[0:R, 1:2]), reads=sso.r, writes=sso.r)
                            on_ = on[i2]
                            b.op("dve", lambda e: e.tensor_scalar_mul(on_[0:R, :], oo[0:R, :], sso[0:R, 1:2]), reads=oo.r + sso.r, writes=on_.r)

                            def tail(on_=on_, g_=g_, R=R, c0=c0, h=h):
                                pt2 = pst[cnt["pt"] % 2]
                                cnt["pt"] += 1
                                for j in range(2):
                                    b.op("pe", lambda e, j=j: e.transpose(pt2[:, j * 128:j * 128 + R], on_[0:R, j * 128:(j + 1) * 128], ident[0:R, 0:R]),
                                         reads=on_.r + ident.r, writes=pt2.r, inc=(j == 1))
                                srcp = pt2[:, 0:256].rearrange("p (j n) -> p j n", j=2)[:, :, 0:R]
                                b.op("dve", lambda e: e.tensor_tensor(mixT[:, 8 + 2 * h:10 + 2 * h, c0:c0 + R], srcp, g_[:, :, 0:R], ALU.mult),
                                     reads=pt2.r + g_.r, writes=mixT.r)

                            pending.append(tail)
                            while len(pending) > 1:
                                pending.pop(0)()
                            if late_r:
                                late_r.pop(0)()
                            if late_r and n % 3 == 0:
                                late_r.pop(0)()
                    while pending:
                        pending.pop(0)()
                    while late_r:
                        late_r.pop(0)()
                b.barrier()
                with ExitStack() as so:
                    Gp = mk(so, "G2p", [128, D], F32)
                    Gs = mk(so, "G2s", [128, D], F32)
                    wo = [mk(so, "wo%d" % i, [128, NKT, 512], BF16) for i in range(2)]
                    dt_ = [mk(so, "odt%d" % i, [128, 512], F32) for i in range(2)]
                    for (src, psl) in mod_row_bc(5, 0, blk):
                        b.dma("sp", Gp[psl, :], src, reads=[modreg], writes=Gp.r)
                    for (src, psl) in mod_row_bc(5, 1, blk):
                        b.dma("sp", Gs[psl, :], src, reads=[modreg], writes=Gs.r)
                    b.dma("pool", wo[0][:], w_out[0], writes=wo[0].r)
                    pi = 0
                    for n4 in range(4):
                        if n4 + 1 < 4:
                            b.dma("pool", wo[(n4 + 1) % 2][:], w_out[n4 + 1], writes=wo[(n4 + 1) % 2].r)
                        w_ = wo[n4 % 2]
                        for t in range(NPT + 1):
                            R = rows_of(t)
                            p = ps[pi % 4]
                            d_ = dt_[pi % 2]
                            pi += 1
                            for k in range(NKT):
                                b.op("pe", lambda e, k=k: e.matmul(p[0:R, :], mixT[:, k, t * 128:t * 128 + R], w_[:, k, :], start=(k == 0), stop=(k == NKT - 1)),
                                     reads=mixT.r + w_.r, writes=p.r, inc=(k == NKT - 1))
                            G_ = Gp if t < NPT else Gs
                            b.op("dve", lambda e: e.tensor_tensor(d_[0:R, :], p[0:R, :], G_[0:R, n4 * 512:(n4 + 1) * 512], ALU.mult), reads=p.r + G_.r, writes=d_.r)
                            b.op("pool", lambda e: e.tensor_tensor(x[0:R, t, n4 * 512:(n4 + 1) * 512], x[0:R, t, n4 * 512:(n4 + 1) * 512], d_[0:R, :], ALU.add),
                                 reads=d_.r + [x.r[4 * t + n4]], writes=[x.r[4 * t + n4]])
                    sqj2 = mk(so, "sqj2", [128, D], BF16)
                    batched_rstd(list(range(NPT + 1)), sqj2)
            b.barrier()

        def final_stage(blk, stats_done=False):
            with ExitStack() as sn:
                A = mk(sn, "fA", [128, D], F32)
                yo = [mk(sn, "fy%d" % i, [128, D], F32) for i in range(3)]
                b.dma("sp", A[:], dap(norms, 3 * D, [[0, 128], [1, D]]), writes=A.r)
                if not stats_done:
                    sqj = mk(sn, "sqj", [128, D], BF16)
                    batched_rstd(list(range(NPT + 1)), sqj)
                for t in range(NPT + 1):
                    R = rows_of(t)
                    yt = yo[t % 3]
                    b.op("dve", lambda e: e.scalar_tensor_tensor(yt[0:R, :], x[0:R, t, :], rstd[0:R, t:t + 1], A[0:R, :], ALU.mult, ALU.mult),
                         reads=x.r[4 * t:4 * t + 4] + rstd.r + A.r, writes=yt.r)
                    if t < NPT:
                        b.dma("sp", y_p[blk * TPB + t * 128: blk * TPB + (t + 1) * 128, :], yt[:], reads=yt.r)
                    else:
                        b.dma("sp", y_s[blk * NS:(blk + 1) * NS, :], yt[0:NS, :], reads=yt.r)
            b.barrier()

        def load_x(pre):
            src = xp_pre if pre else xp
            for t in range(NPT):
                b.dma("sp", x[:, t, :], src[t * 128:(t + 1) * 128, :], writes=x.r[4 * t:4 * t + 4])
            if not pre:
                b.dma("sp", x[0:NS, NPT, :], xs[0:NS, :], writes=x.r[4 * NPT:4 * NPT + 4])

        ffn_stage(0, 0, 1, 2, 0, 0, sample=False, late_spec=(0, 32, 3 * D), stats_done=True, next_tiles=list(range(NPT)))
        mixer_lite()
        ffn_stage(0, 0, 1, 2, 0, 0, late_spec=(32, 16, 5 * D), next_tiles=list(range(NPT + 1)))
        mixer_stage(0)
        ffn_stage(1, 6, 7, 8, 2, 0, stats_done=True, next_tiles=list(range(NPT + 1)))
        final_stage(0, stats_done=True)
        b.barrier()
    return nc


def _consts():
    half = 128
    inv = (10000.0 ** (-np.arange(half, dtype=np.float32) / np.float32(half))).astype(np.float32)
    c = {}
    pos_p = np.arange(2 * TPB, dtype=np.float32)
    ang = (pos_p[:, None] * inv[None, :]).astype(np.float32)
    cos_all = np.ascontiguousarray(np.cos(ang).T.reshape(128, 2, TPB).transpose(1, 0, 2)).astype(np.float32)
    sin_all = np.ascontiguousarray(np.sin(ang).T.reshape(128, 2, TPB).transpose(1, 0, 2)).astype(np.float32)
    c["cos_all"], c["sin_all"] = cos_all, sin_all
    tt = np.repeat(np.arange(4), NSB)
    pos_s = (16384.0 + tt).astype(np.float32)
    angs = (pos_s[:, None] * inv[None, :]).astype(np.float32)
    c["c_coss"] = np.ascontiguousarray(np.cos(angs).T).astype(np.float32)
    c["c_sins"] = np.ascontiguousarray(np.sin(angs).T).astype(np.float32)
    c["c_ident"] = np.eye(128, dtype=np.float32)
    gam = np.array(GAM, dtype=np.float64)
    idx = np.arange(128)
    diff = idx[None, :] - idx[:, None]
    maskp = np.zeros((128, 4, 128), np.float64)
    for h in range(4):
        maskp[:, h, :] = np.where(diff >= 0, gam[h] ** np.maximum(diff, 0), 0.0) / 16.0
    c["c_maskp"] = maskp.astype(np.float32)
    ps_ = np.arange(NS)
    tt_, bb_ = ps_ // NSB, ps_ % NSB
    dts = tt_[None, :] - tt_[:, None]
    same = (bb_[None, :] == bb_[:, None])
    masks = np.zeros((NS, 4, NS), np.float64)
    for h in range(4):
        masks[:, h, :] = np.where(same & (dts >= 0), gam[h] ** np.maximum(dts, 0), 0.0) / 16.0
    c["c_masks"] = masks.astype(np.float32)
    qdec = np.zeros((128, 8), np.float64)
    kdec = np.zeros((128, 8), np.float64)
    for h in range(4):
        qdec[:, h] = gam[h] ** (idx + 1.0)
        kdec[:, h] = gam[h] ** (127.0 - idx) / 16.0
        qdec[:NS, 4 + h] = gam[h] ** (tt_ + 1.0)
        kdec[:NS, 4 + h] = gam[h] ** (3.0 - tt_) / 16.0
    kdn = np.zeros((128, 32), np.float64)
    for n in range(8):
        for h in range(4):
            kdn[:, n * 4 + h] = gam[h] ** (127.0 - idx) * gam[h] ** (128.0 * (7 - n)) / 16.0
    c["c_kdecn"] = kdn.astype(np.float32)
    c["c_qdec"] = qdec.astype(np.float32)
    c["c_kdec"] = kdec.astype(np.float32)
    bm = (bb_[None, :] == np.arange(NSB)[:, None]).astype(np.float32)
    c["c_bmask"] = np.ascontiguousarray(np.broadcast_to(bm[None], (128, NSB, NS))).astype(np.float32)
    c["c_rmask"] = np.ascontiguousarray(bm.T).astype(np.float32)
    invc = np.zeros((2, 128, 4, 16), np.float32)
    for par in range(2):
        for g in range(4):
            pos = par * TPB + np.arange(16)
            invc[par, :, g, :] = (1.0 / np.minimum(pos + 1.0, float(2 ** (g + 1))))[None, :]
    c["invc_all"] = invc
    return c


def make_in_maps(inputs):
    g = lambda k: np.asarray(inputs[k], dtype=np.float32)
    cst = _consts()
    cos_all, sin_all, invc_all = cst.pop("cos_all"), cst.pop("sin_all"), cst.pop("invc_all")
    x_prompt, x_sample = g("x_prompt"), g("x_sample")
    c_prompt, c_sample = g("c_prompt"), g("c_sample")
    state_pool, state_ret = g("state_pool")[0], g("state_ret")[0]
    norms = np.stack([g("norm_ffn1")[0], g("norm_mix")[0], g("norm_ffn2")[0], g("norm_final")], 0)

    def ktile(w, cw):
        n = w.shape[1]
        return np.ascontiguousarray(w.reshape(NKT, 128, n // cw, cw).transpose(2, 1, 0, 3))

    w_in_full = g("w_in")[0]
    shared = {
        "ada_w": ktile(np.ascontiguousarray(g("ada_w")[0][:, :3 * D]), 512), "ada_wb": ktile(np.ascontiguousarray(g("ada_w")[0][:, 3 * D:]), 128), "ada_b": g("ada_b"), "norms": np.ascontiguousarray(norms),
        "wg1": ktile(g("ffn1_w_gate")[0], 128), "wu1": ktile(g("ffn1_w_up")[0], 128), "wd1": g("ffn1_w_down")[0],
        "wg2": ktile(g("ffn2_w_gate")[0], 128), "wu2": ktile(g("ffn2_w_up")[0], 128), "wd2": g("ffn2_w_down")[0],
        "w_in": ktile(w_in_full, 128), "w_inv": ktile(np.ascontiguousarray(w_in_full[:, 3072:4096]), 512),
        "pool_w": g("pool_w")[0],
        "pool_sc": np.ascontiguousarray(g("pool_scale")[0].reshape(8, 128).T),
        "w_out": ktile(g("w_out")[0], 512),
    }
    shared.update(cst)
    zeros_pre = np.zeros((TPB, D), np.float32)
    in_maps = []
    for c in range(NCORE):
        sq, par = c // 2, c % 2
        xs = x_sample[c * NSC:(c + 1) * NSC]
        xs = np.ascontiguousarray(xs.transpose(1, 0, 2).reshape(NS, D))
        cT = np.concatenate([c_prompt[sq:sq + 1], c_sample[c * NSC:(c + 1) * NSC]], 0).T
        m = dict(shared)
        m.update({
            "xp": np.ascontiguousarray(x_prompt[sq, par * TPB:(par + 1) * TPB]),
            "xp_pre": np.ascontiguousarray(x_prompt[sq, 0:TPB]) if par == 1 else zeros_pre,
            "flag": np.full((128, 1), float(par), np.float32),
            "xs": xs,
            "cT": np.ascontiguousarray(cT),
            "spool": np.ascontiguousarray(state_pool[c * NSC:(c + 1) * NSC]),
            "sret": np.ascontiguousarray(state_ret[c * NSC:(c + 1) * NSC]),
            "c_cosp": np.ascontiguousarray(np.stack([cos_all[0], cos_all[par]], 0)),
            "c_sinp": np.ascontiguousarray(np.stack([sin_all[0], sin_all[par]], 0)),
            "c_invc": np.ascontiguousarray(invc_all[par:par + 1]),
        })
        in_maps.append(m)
    return in_maps


_NC_CACHE = {}


def kernel(**inputs):
    in_maps = make_in_maps(inputs)
    if "nc" not in _NC_CACHE:
        _NC_CACHE["nc"] = build_program()
    nc = _NC_CACHE["nc"]
    res = run_bass_kernel_spmd(nc, in_maps, core_ids=list(range(NCORE)))
    return assemble(res.results)


def assemble(results):
    y_p = np.stack([np.concatenate([results[2 * s]["y_p"], results[2 * s + 1]["y_p"]], 0) for s in range(4)], 0)
    ys = [results[c]["y_s"].reshape(4, NSB, D).transpose(1, 0, 2) for c in range(NCORE)]
    y_s = np.concatenate(ys, 0)
    pool_p = np.stack([results[2 * s + 1]["o_poolp"][1:16] for s in range(4)], 0)[None]
    ret_p = np.stack([results[2 * s + 1]["o_retp"] for s in range(4)], 0)[None]
    pool_s = np.concatenate([results[c]["o_pools"] for c in range(NCORE)], 0)[None]
    ret_s = np.concatenate([results[c]["o_rets"] for c in range(NCORE)], 0)[None]
    return (y_p.astype(np.float32), y_s.astype(np.float32), pool_p.astype(np.float32),
            ret_p.astype(np.float32), pool_s.astype(np.float32), ret_s.astype(np.float32))
```

```python
import numpy as np
from contextlib import ExitStack
import concourse.bass as bass
import concourse.mybir as mybir
from concourse.bass_utils import run_bass_kernel_spmd

F32 = mybir.dt.float32
BF16 = mybir.dt.bfloat16
ALU = mybir.AluOpType
AF = mybir.ActivationFunctionType

D = 2048
DFF = 5632
NKT = D // 128
NFT = DFF // 128
NMOD = 9
INC = 5120
NB = 1
NPT = 8
TPB = 1024
NSB = 16
NS = 64
TB = TPB + NS
NCORE = 8
NSC = 16
EPS = 1e-6
FG = 4
NG = NFT // FG
GAM = [1.0 - 2.0 ** (-5.0 - h) for h in range(4)]


def dap(t, offset, pat):
    return bass.AP(t.tensor, offset, pat)


class Reg:
    __slots__ = ("name", "w", "r")

    def __init__(self, name):
        self.name = name
        self.w = None
        self.r = {}


class Bld:
    def __init__(self, nc, es):
        self.nc = nc
        self.E = {"pe": nc.tensor, "act": nc.scalar, "dve": nc.vector, "pool": nc.gpsimd, "sp": nc.sync}
        self.sem, self.cnt = {}, {}
        self.waited = {e: {} for e in self.E}
        for e in self.E:
            self.sem[e] = es.enter_context(nc.semaphore("s_" + e))
            self.cnt[e] = 0
        self.dsem, self.dpos = {}, {}
        for q, n in (("sp", 24), ("pool", 24), ("act", 8)):
            self.dsem[q] = [[es.enter_context(nc.semaphore("d_%s%d" % (q, i))), 0] for i in range(n)]
            self.dpos[q] = 0
        self.regs = []

    def reg(self, name):
        r = Reg(name)
        self.regs.append(r)
        return r

    def _wait(self, e, sem, val):
        k = id(sem)
        if self.waited[e].get(k, 0) >= val:
            return
        self.E[e].wait_ge(sem, val)
        self.waited[e][k] = val

    def _deps(self, reads, writes):
        evs = []
        for r in reads:
            if r.w is not None:
                evs.append(r.w)
        for w in writes:
            if w.w is not None:
                evs.append(w.w)
            evs.extend(w.r.values())
        return evs

    def _record(self, ev, reads, writes):
        k = id(ev[0])
        for r in reads:
            o = r.r.get(k)
            if o is None or o[1] < ev[1]:
                r.r[k] = ev
        for w in writes:
            w.w = ev
            w.r = {}

    def op(self, e, fn, reads=(), writes=(), inc=True):
        for ev in self._deps(reads, writes):
            if ev[2] == "pe" and e == "pe":
                continue
            self._wait(e, ev[0], ev[1])
        ins = fn(self.E[e])
        if inc:
            ins.then_inc(self.sem[e], 1)
            self.cnt[e] += 1
            ev = (self.sem[e], self.cnt[e], e)
        else:
            ev = (self.sem[e], self.cnt[e] + 1, e)
        self._record(ev, reads, writes)
        return ev

    def dma(self, q, out, in_, reads=(), writes=()):
        for ev in self._deps(reads, writes):
            self._wait(q, ev[0], ev[1])
        pool = self.dsem[q]
        slot = pool[self.dpos[q] % len(pool)]
        self.dpos[q] += 1
        if slot[1] > 0:
            self._wait(q, slot[0], slot[1])
        ins = self.E[q].dma_start(out=out, in_=in_)
        slot[1] += 16
        ins.then_inc(slot[0], 16)
        ev = (slot[0], slot[1], "dma")
        self._record(ev, reads, writes)
        return ev

    def barrier(self):
        evs = [(self.sem[e], self.cnt[e]) for e in self.E if self.cnt[e] > 0]
        for q in self.dsem:
            for s in self.dsem[q]:
                if s[1] > 0:
                    evs.append((s[0], s[1]))
        for e in self.E:
            for sem, val in evs:
                self._wait(e, sem, val)
        for r in self.regs:
            r.w = None
            r.r = {}


class Tile:
    _uid = [0]

    def __init__(self, b, es, name, shape, dtype, nreg=1, psum=False):
        Tile._uid[0] += 1
        name = "%s_%d" % (name, Tile._uid[0])
        if psum:
            self.t = es.enter_context(b.nc.psum_tensor(name, list(shape), dtype))
        else:
            self.t = es.enter_context(b.nc.sbuf_tensor(name, list(shape), dtype))
        self.r = [b.reg("%s_%d" % (name, i)) for i in range(nreg)]

    def __getitem__(self, k):
        return self.t[k]


class View:
    def __init__(self, ap, regs):
        self.t = ap
        self.r = regs

    def __getitem__(self, k):
        return self.t[k]


def build_program(stop_after=None):
    nc = bass.Bass("TRN2", target_bir_lowering=False)

    def din(name, shape, dt=F32):
        return nc.dram_tensor(name, list(shape), dt, kind="ExternalInput").ap()

    def dout(name, shape, dt=F32):
        return nc.dram_tensor(name, list(shape), dt, kind="ExternalOutput").ap()

    xp = din("xp", [NB * TPB, D])
    xp_pre = din("xp_pre", [TPB, D])
    flag = din("flag", [128, 1])
    xs = din("xs", [NB * NS, D])
    cT = din("cT", [D, 1 + NSC])
    spool = din("spool", [NSC, 15, 1024])
    sret = din("sret", [NSC, 4, 256, 256])
    ada_w = din("ada_w", [3 * D // 512, 128, NKT, 512])
    ada_wb = din("ada_wb", [6 * D // 128, 128, NKT, 128])
    ada_b = din("ada_b", [1, NMOD * D])
    norms = din("norms", [4, D])
    wg = [din("wg1", [NFT, 128, NKT, 128]), din("wg2", [NFT, 128, NKT, 128])]
    wu = [din("wu1", [NFT, 128, NKT, 128]), din("wu2", [NFT, 128, NKT, 128])]
    wd = [din("wd1", [DFF, D]), din("wd2", [DFF, D])]
    w_in = din("w_in", [INC // 128, 128, NKT, 128])
    w_inv = din("w_inv", [2, 128, NKT, 512])
    pool_w = din("pool_w", [4, 256, 256])
    pool_sc = din("pool_sc", [128, 8])
    w_out = din("w_out", [4, 128, NKT, 512])
    c_ident = din("c_ident", [128, 128])
    c_cosp = din("c_cosp", [2, 128, TPB])
    c_sinp = din("c_sinp", [2, 128, TPB])
    c_coss = din("c_coss", [128, NS])
    c_sins = din("c_sins", [128, NS])
    c_maskp = din("c_maskp", [128, 4, 128])
    c_masks = din("c_masks", [NS, 4, NS])
    c_qdec = din("c_qdec", [128, 8])
    c_kdec = din("c_kdec", [128, 8])
    c_kdecn = din("c_kdecn", [128, 32])
    c_bmask = din("c_bmask", [128, NSB, NS])
    c_rmask = din("c_rmask", [NS, NSB])
    c_invc = din("c_invc", [NB, 128, 4, 16])

    y_p = dout("y_p", [NB * TPB, D])
    y_s = dout("y_s", [NB * NS, D])
    o_poolp = dout("o_poolp", [16, 1024])
    o_retp = dout("o_retp", [4, 256, 256])
    o_pools = dout("o_pools", [NSC, 15, 1024])
    o_rets = dout("o_rets", [NSC, 4, 256, 256])

    mods_d = nc.dram_tensor("mods_d", [1 + NSC, NMOD * D], F32).ap()
    featT_d = nc.dram_tensor("featT_d", [INC, TB], F32).ap()
    S_d = nc.dram_tensor("S_d", [128, 4, 2, 256], F32).ap()
    utail_d = nc.dram_tensor("utail_d", [128, 8, 16], F32).ap()

    with ExitStack() as es:
        b = Bld(nc, es)
        mk = lambda st, name, shape, dt, nreg=1, psum=False: Tile(b, st, name, shape, dt, nreg, psum)
        featreg = b.reg("featT_d")
        sdreg = b.reg("S_d")
        udreg = b.reg("utail_d")

        x = mk(es, "x", [128, NPT + 1, D], F32, nreg=4 * (NPT + 1))
        ident = mk(es, "ident", [128, 128], BF16)
        identf = mk(es, "identf", [128, 128], F32)
        ss = mk(es, "ss", [128, 16], F32)
        rstd = mk(es, "rstd", [128, 16], F32)
        qdec = mk(es, "qdec", [128, 8], F32)
        kdec = mk(es, "kdec", [128, 8], F32)
        psc = mk(es, "psc", [128, 8], F32)
        flg = mk(es, "flg", [128, 1], F32)
        ps = [mk(es, "ps%d" % i, [128, 512], F32, psum=True) for i in range(4)]
        psd = [mk(es, "psd%d" % i, [128, 1024], F32, nreg=2, psum=True) for i in range(2)]
        ps = ps + [View(psd[i][:, 512:1024], [psd[i].r[1]]) for i in range(2)]
        pst = [View(psd[i][:, 0:512].bitcast(BF16), [psd[i].r[0]]) for i in range(2)]

        b.dma("pool", ident[:], c_ident, writes=ident.r)
        b.dma("sp", identf[:], c_ident, writes=identf.r)
        b.dma("sp", qdec[:], c_qdec, writes=qdec.r)
        b.dma("sp", kdec[:], c_kdec, writes=kdec.r)
        b.dma("sp", psc[:], pool_sc, writes=psc.r)
        b.dma("sp", flg[:], flag, writes=flg.r)
        b.op("dve", lambda e: e.memset(ss[:], 1.0), writes=ss.r)

        def tiles_of(grp):
            return list(range(NPT)) if grp == 0 else [NPT]

        def rows_of(t):
            return 128 if t < NPT else NS

        def batched_rstd(tl, sqj, view=None):
            for t in tl:
                R = rows_of(t)
                jv = sqj[0:R, :] if view is None else view[0:R, :]
                b.op("act", lambda e: e.activation(jv, x[0:R, t, :], AF.Square, accum_out=ss[0:R, t:t + 1]),
                     reads=x.r[4 * t:4 * t + 4], writes=sqj.r + ss.r)
            nt = max(tl) + 1
            b.op("dve", lambda e: e.tensor_scalar(rstd[:, 0:nt], ss[:, 0:nt], 1.0 / D, EPS, ALU.mult, ALU.add), reads=ss.r, writes=rstd.r)
            b.op("act", lambda e: e.sqrt(rstd[:, 0:nt], rstd[:, 0:nt]), reads=rstd.r, writes=rstd.r)
            b.op("dve", lambda e: e.reciprocal(rstd[:, 0:nt], rstd[:, 0:nt]), reads=rstd.r, writes=rstd.r)

        csil = mk(es, "csil", [128, NKT, 1 + NSC], BF16)
        modreg = b.reg("mods_d")
        NCH = NMOD * D // 512
        NCH_A = 3 * D // 512
        with ExitStack() as s00:
            ctile = mk(s00, "ctile", [128, NKT, 1 + NSC], F32)
            b.dma("sp", ctile[:], cT.rearrange("(k p) m -> p k m", p=128), writes=ctile.r)
            b.op("act", lambda e: e.activation(csil[:], ctile[:], AF.Silu), reads=ctile.r, writes=csil.r)
        b.barrier()

        def mods_emitters(st, src, nch, cw, col0, psl, nring, src_off=0):
            aw = [mk(st, "aw%d" % i, [128, NKT, cw], BF16) for i in range(nring)]
            mo = [mk(st, "mo%d" % i, [1 + NSC, cw], F32) for i in range(2)]
            bb = [mk(st, "bb%d" % i, [1 + NSC, cw], F32) for i in range(2)]
            M = 1 + NSC

            def load_aw(i):
                b.dma("pool", aw[i % nring][:], src[src_off + i], writes=aw[i % nring].r)

            def emit(i):
                if i == 0:
                    for i2 in range(0, min(nring - 1, nch)):
                        load_aw(i2)
                if i + nring - 1 < nch:
                    load_aw(i + nring - 1)
                a = aw[i % nring]
                p = psl[i % len(psl)]
                bt = bb[i % 2]
                mt = mo[i % 2]
                c0 = col0 + i * cw
                b.dma("sp", bt[:], dap(ada_b, c0, [[0, M], [1, cw]]), writes=bt.r)
                for k in range(NKT):
                    b.op("pe", lambda e, k=k: e.matmul(p[0:M, 0:cw], csil[:, k, :], a[:, k, :], start=(k == 0), stop=(k == NKT - 1)),
                         reads=csil.r + a.r, writes=p.r, inc=(k == NKT - 1))
                b.op("dve", lambda e: e.tensor_tensor(mt[:], p[0:M, 0:cw], bt[:], ALU.add), reads=p.r + bt.r, writes=mt.r)
                b.dma("sp", mods_d[:, c0:c0 + cw], mt[:], reads=mt.r, writes=[modreg])

            return [(lambda i=i: emit(i)) for i in range(nch)]

        with ExitStack() as s0:
            for t in range(NPT):
                b.dma("sp", x[:, t, :], xp_pre[t * 128:(t + 1) * 128, :], writes=x.r[4 * t:4 * t + 4])
            sqj0 = mk(s0, "sqj0", [128, D], BF16)
            batched_rstd(list(range(NPT)), sqj0)
            for em in mods_emitters(s0, ada_w, NCH_A, 512, 0, [ps[0], ps[1]], 3):
                em()
        b.barrier()

        def mod_row_bc(sec, grp, blk):
            if grp == 0:
                return [(dap(mods_d, sec * D, [[0, 128], [1, D]]), slice(0, 128))]
            res = []
            for t in range(4):
                res.append((dap(mods_d, (1 + blk * NSB) * NMOD * D + sec * D, [[NMOD * D, NSB], [1, D]]),
                            slice(t * NSB, (t + 1) * NSB)))
            return res

        def norm_stage(st, hT, gain_idx, sec_sh, sec_sc, blk, sample=True, stats_done=False):
            with ExitStack() as sn:
                groups = (0, 1) if sample else (0,)
                A = [mk(sn, "nA%d" % g_, [128, D], F32) for g_ in groups]
                Bt = [mk(sn, "nB%d" % g_, [128, D], F32) for g_ in groups]
                gt = mk(sn, "ngt", [128, D], F32)
                tmp = mk(sn, "ntmp", [128, D], F32)
                hb = [mk(sn, "nhb%d" % i, [128, D], BF16) for i in range(2)]
                b.dma("sp", gt[:], dap(norms, gain_idx * D, [[0, 128], [1, D]]), writes=gt.r)
                for grp in groups:
                    for (src, psl) in mod_row_bc(sec_sc, grp, blk):
                        b.dma("sp", A[grp][psl, :], src, reads=[modreg], writes=A[grp].r)
                    for (src, psl) in mod_row_bc(sec_sh, grp, blk):
                        b.dma("sp", Bt[grp][psl, :], src, reads=[modreg], writes=Bt[grp].r)
                tl = [t for grp in groups for t in tiles_of(grp)]
                if not stats_done:
                    sqj = mk(sn, "sqj", [128, D], BF16)
                    batched_rstd(tl, sqj)
                for grp in groups:
                    Rg = 128 if grp == 0 else NS
                    b.op("dve", lambda e: e.scalar_tensor_tensor(A[grp][0:Rg, :], A[grp][0:Rg, :], 1.0, gt[0:Rg, :], ALU.add, ALU.mult),
                         reads=A[grp].r + gt.r, writes=A[grp].r)
                cnt = 0
                for grp in groups:
                    for t in tiles_of(grp):
                        R = rows_of(t)
                        h = hb[cnt % 2]
                        cnt += 1
                        b.op("dve", lambda e: e.scalar_tensor_tensor(tmp[0:R, :], x[0:R, t, :], rstd[0:R, t:t + 1], A[grp][0:R, :], ALU.mult, ALU.mult),
                             reads=x.r[4 * t:4 * t + 4] + rstd.r + A[grp].r, writes=tmp.r)
                        b.op("dve", lambda e: e.tensor_tensor(h[0:R, :], tmp[0:R, :], Bt[grp][0:R, :], ALU.add),
                             reads=tmp.r + Bt[grp].r, writes=h.r)
                        for kq in range(4):
                            pt = pst[kq % 2]
                            for kk in range(4):
                                k = kq * 4 + kk
                                b.op("pe", lambda e, k=k, kk=kk: e.transpose(pt[:, kk * 128:kk * 128 + R], h[0:R, k * 128:(k + 1) * 128], ident[0:R, 0:R]),
                                     reads=h.r + ident.r, writes=pt.r, inc=(kk == 3))
                            src = pt[:, 0:512].rearrange("p (a n) -> p a n", a=4)[:, :, 0:R]
                            b.op("act", lambda e, kq=kq, src=src: e.copy(hT[:, kq * 4:(kq + 1) * 4, t * 128:t * 128 + R], src),
                                 reads=pt.r, writes=[hT.r[t]])
            b.barrier()

        def ffn_stage(wi, sec_sh, sec_sc, sec_gt, gain_idx, blk, sample=True, late_spec=None, stats_done=False, next_tiles=None):
            with ExitStack() as sf:
                hT = mk(sf, "hT", [128, NKT, TB], BF16, nreg=NPT + 1)
                norm_stage(sf, hT, gain_idx, sec_sh, sec_sc, blk, sample, stats_done)
                Gp = mk(sf, "Gp", [128, D], F32)
                Gs = mk(sf, "Gs", [128, D], F32)
                NWR = 2
                wgr = [mk(sf, "wgr%d" % i, [128, NKT, 128], BF16) for i in range(NWR)]
                wur = [mk(sf, "wur%d" % i, [128, NKT, 128], BF16) for i in range(NWR)]
                wdr = [mk(sf, "wdr%d" % i, [128, FG, D], BF16, nreg=FG) for i in range(2)]
                aT = [mk(sf, "aT%d" % i, [128, FG, TB], BF16) for i in range(2)]
                sg = [mk(sf, "sg%d" % i, [128, 512], F32) for i in range(2)]
                dtmp = [mk(sf, "dtmp%d" % i, [128, 512], F32) for i in range(1)]
                late = []
                if late_spec is not None:
                    (l_lo, l_n, l_col0) = late_spec
                    late = mods_emitters(sf, ada_wb, l_n, 128, l_col0, [ps[4], ps[5]], 2, src_off=l_lo)
                n_late = len(late)
                for (src, psl) in mod_row_bc(sec_gt, 0, blk):
                    b.dma("sp", Gp[psl, :], src, reads=[modreg], writes=Gp.r)
                for (src, psl) in mod_row_bc(sec_gt, 1, blk):
                    b.dma("sp", Gs[psl, :], src, reads=[modreg], writes=Gs.r)
                b.op("dve", lambda e: e.tensor_scalar_mul(Gp[:], Gp[:], 0.5), reads=Gp.r, writes=Gp.r)
                b.op("dve", lambda e: e.tensor_scalar_mul(Gs[0:NS, :], Gs[0:NS, :], 0.5), reads=Gs.r, writes=Gs.r)
                wdv = wd[wi].rearrange("(f p) n -> p f n", p=128)

                def load_up(f):
                    b.dma("pool", wgr[f % NWR][:], wg[wi][f], writes=wgr[f % NWR].r)
                    b.dma("pool", wur[f % NWR][:], wu[wi][f], writes=wur[f % NWR].r)

                def load_down(j):
                    b.dma("pool", wdr[j % 2][:], wdv[:, j * FG:(j + 1) * FG, :], writes=wdr[j % 2].r)

                tcs = [(0, 512), (512, 512), (1024, NS)] if sample else [(0, 512), (512, 512)]
                state = {"pi": 0, "sgi": 0, "dp": 0, "dt": 0}

                def up(j):
                    at = aT[j % 2]
                    for fi in range(FG):
                        f = j * FG + fi
                        want = (n_late * (f + 1)) // NFT
                        while late and n_late - len(late) < want:
                            late.pop(0)()
                        if f + NWR - 1 < NFT:
                            load_up(f + NWR - 1)
                        g_w = wgr[f % NWR]
                        u_w = wur[f % NWR]
                        for (c0, cn) in tcs:
                            pg = ps[state["pi"] % 4]
                            pu = ps[(state["pi"] + 1) % 4]
                            state["pi"] += 2
                            for k in range(NKT):
                                b.op("pe", lambda e, k=k: e.matmul(pg[:, 0:cn], g_w[:, k, :], hT[:, k, c0:c0 + cn], start=(k == 0), stop=(k == NKT - 1)),
                                     reads=g_w.r + hT.r, writes=pg.r, inc=(k == NKT - 1))
                            for k in range(NKT):
                                b.op("pe", lambda e, k=k: e.matmul(pu[:, 0:cn], u_w[:, k, :], hT[:, k, c0:c0 + cn], start=(k == 0), stop=(k == NKT - 1)),
                                     reads=u_w.r + hT.r, writes=pu.r, inc=(k == NKT - 1))
                            s_ = sg[state["sgi"] % 2]
                            state["sgi"] += 1
                            b.op("act", lambda e: e.activation(s_[:, 0:cn], pg[:, 0:cn], AF.Silu), reads=pg.r, writes=s_.r)
                            b.op("dve", lambda e: e.tensor_tensor(at[:, fi, c0:c0 + cn], s_[:, 0:cn], pu[:, 0:cn], ALU.mult),
                                 reads=s_.r + pu.r, writes=at.r)

                def down_sample(j):
                    at = aT[j % 2]
                    w = wdr[j % 2]
                    for n4 in (range(4) if sample else ()):
                        p = ps[4 + state["dp"] % 2]
                        state["dp"] += 1
                        for fi in range(FG):
                            b.op("pe", lambda e, fi=fi: e.matmul(p[0:NS, :], at[:, fi, TPB:TB], w[:, fi, n4 * 512:(n4 + 1) * 512], start=(fi == 0), stop=(fi == FG - 1)),
                                 reads=at.r + [w.r[fi]], writes=p.r, inc=(fi == FG - 1))
                        d_ = dtmp[0]
                        state["dt"] += 1
                        b.op("dve", lambda e: e.tensor_tensor(d_[0:NS, :], p[0:NS, :], Gs[0:NS, n4 * 512:(n4 + 1) * 512], ALU.mult),
                             reads=p.r + Gs.r, writes=d_.r)
                        b.op("pool", lambda e: e.tensor_tensor(x[0:NS, NPT, n4 * 512:(n4 + 1) * 512], x[0:NS, NPT, n4 * 512:(n4 + 1) * 512], d_[0:NS, :], ALU.add),
                             reads=d_.r + [x.r[4 * NPT + n4]], writes=[x.r[4 * NPT + n4]])
                    for fi in range(FG):
                        b.op("dve", lambda e, fi=fi: e.tensor_tensor(w[:, fi, :], w[:, fi, :], Gp[:], ALU.mult),
                             reads=[w.r[fi]] + Gp.r, writes=[w.r[fi]])

                def down_prompt(j):
                    at = aT[j % 2]
                    w = wdr[j % 2]
                    for t in range(NPT):
                        for n2 in range(2):
                            p = psd[state["dp"] % 2]
                            state["dp"] += 1
                            for hf in range(2):
                                n4 = 2 * n2 + hf
                                for fi in range(FG):
                                    b.op("pe", lambda e, fi=fi: e.matmul(p[:, hf * 512:(hf + 1) * 512], at[:, fi, t * 128:(t + 1) * 128], w[:, fi, n4 * 512:(n4 + 1) * 512], start=(fi == 0), stop=(fi == FG - 1)),
                                         reads=at.r + [w.r[fi]], writes=[p.r[hf]], inc=(fi == FG - 1))
                            b.op("dve", lambda e: e.tensor_tensor(x[:, t, n2 * 1024:(n2 + 1) * 1024], x[:, t, n2 * 1024:(n2 + 1) * 1024], p[:, :], ALU.add),
                                 reads=p.r + x.r[4 * t + 2 * n2:4 * t + 2 * n2 + 2], writes=x.r[4 * t + 2 * n2:4 * t + 2 * n2 + 2])

                for f in range(NWR - 1):
                    load_up(f)
                load_down(0)
                up(0)
                down_sample(0)
                for j in range(NG):
                    if j + 1 < NG:
                        up(j + 1)
                        load_down(j + 1)
                    down_prompt(j)
                    if j + 1 < NG:
                        down_sample(j + 1)
                if next_tiles is not None:
                    jv = aT[(NG - 2) % 2][:, 0:2, :].rearrange("p a n -> p (a n)")[:, 0:D]
                    batched_rstd(next_tiles, aT[(NG - 2) % 2], jv)
            b.barrier()


        def rotary_pair(xa, xb, cosT, sinT, rt, tcs):
            for ci, (c0, cn) in enumerate(tcs):
                t1, t2, t3, t4 = rt
                b.op("dve", lambda e: e.tensor_tensor(t1[:, 0:cn], xa[:, c0:c0 + cn], cosT[:, c0:c0 + cn], ALU.mult), reads=[xa.r[ci]] + cosT.r, writes=t1.r)
                b.op("dve", lambda e: e.tensor_tensor(t2[:, 0:cn], xb[:, c0:c0 + cn], sinT[:, c0:c0 + cn], ALU.mult), reads=[xb.r[ci]] + sinT.r, writes=t2.r)
                b.op("pool", lambda e: e.tensor_tensor(t3[:, 0:cn], xb[:, c0:c0 + cn], cosT[:, c0:c0 + cn], ALU.mult), reads=[xb.r[ci]] + cosT.r, writes=t3.r)
                b.op("pool", lambda e: e.tensor_tensor(t4[:, 0:cn], xa[:, c0:c0 + cn], sinT[:, c0:c0 + cn], ALU.mult), reads=[xa.r[ci]] + sinT.r, writes=t4.r)
                b.op("dve", lambda e: e.tensor_tensor(xa[:, c0:c0 + cn], t1[:, 0:cn], t2[:, 0:cn], ALU.subtract), reads=t1.r + t2.r, writes=[xa.r[ci]])
                b.op("pool", lambda e: e.tensor_tensor(xb[:, c0:c0 + cn], t3[:, 0:cn], t4[:, 0:cn], ALU.add), reads=t3.r + t4.r, writes=[xb.r[ci]])

        def mixer_lite():
            with ExitStack() as sm:
                v = mk(sm, "v", [128, NPT + 1, 1024], BF16, nreg=NPT + 1)
                tcs = [(0, 512), (512, 512)]
                with ExitStack() as sa:
                    hT = mk(sa, "hT", [128, NKT, TB], BF16, nreg=NPT + 1)
                    norm_stage(sa, hT, 1, 3, 4, 0, sample=False, stats_done=True)
                    load_x(False)
                    NWI = 3
                    wi = [mk(sa, "wi%d" % i, [128, NKT, 128], BF16) for i in range(NWI)]
                    wv = [mk(sa, "wv%d" % i, [128, NKT, 512], BF16) for i in range(2)]
                    stg = [mk(sa, "stg%d" % i, [128, TB], F32, nreg=3) for i in range(4)]
                    cosT = mk(sa, "cosT", [128, TPB], F32)
                    sinT = mk(sa, "sinT", [128, TPB], F32)
                    rt = [mk(sa, "rt%d" % i, [128, 512], F32) for i in range(4)]
                    b.dma("sp", cosT[:], c_cosp[0], writes=cosT.r)
                    b.dma("sp", sinT[:], c_sinp[0], writes=sinT.r)
                    cts = list(range(0, 8)) + list(range(16, 24))

                    def load_wi(i):
                        b.dma("pool", wi[i % NWI][:], w_in[cts[i]], writes=wi[i % NWI].r)

                    for vc in range(2):
                        b.dma("pool", wv[vc][:], w_inv[vc], writes=wv[vc].r)
                    for i in range(NWI - 1):
                        load_wi(i)
                    pi = 0
                    for i, ct in enumerate(cts):
                        if i + NWI - 1 < len(cts):
                            load_wi(i + NWI - 1)
                        w_ = wi[i % NWI]
                        st_ = stg[i % 4]
                        for ci, (c0, cn) in enumerate(tcs):
                            p = ps[pi % 4]
                            pi += 1
                            for k in range(NKT):
                                b.op("pe", lambda e, k=k: e.matmul(p[:, 0:cn], w_[:, k, :], hT[:, k, c0:c0 + cn], start=(k == 0), stop=(k == NKT - 1)),
                                     reads=w_.r + hT.r, writes=p.r, inc=(k == NKT - 1))
                            b.op("act", lambda e: e.copy(st_[:, c0:c0 + cn], p[:, 0:cn]), reads=p.r, writes=[st_.r[ci]])
                        if ct < 16:
                            b.dma("sp", featT_d[ct * 128:(ct + 1) * 128, 0:TPB], st_[:, 0:TPB], reads=st_.r, writes=[featreg])
                        elif ct % 2 == 1:
                            sa_p = stg[(i - 1) % 4]
                            rotary_pair(sa_p, st_, cosT, sinT, rt, tcs)
                            b.dma("sp", featT_d[(ct - 1) * 128:ct * 128, 0:TPB], sa_p[:, 0:TPB], reads=sa_p.r, writes=[featreg])
                            b.dma("sp", featT_d[ct * 128:(ct + 1) * 128, 0:TPB], st_[:, 0:TPB], reads=st_.r, writes=[featreg])
                    for vc in range(2):
                        for t in range(NPT):
                            p = ps[4 + (pi % 2)]
                            pi += 1
                            for k in range(NKT):
                                b.op("pe", lambda e, k=k: e.matmul(p[:, :], hT[:, k, t * 128:(t + 1) * 128], wv[vc][:, k, :], start=(k == 0), stop=(k == NKT - 1)),
                                     reads=wv[vc].r + hT.r, writes=p.r, inc=(k == NKT - 1))
                            if t % 2 == 0:
                                b.op("act", lambda e: e.copy(v[:, t, vc * 512:(vc + 1) * 512], p[:, :]), reads=p.r, writes=[v.r[t]])
                            else:
                                b.op("dve", lambda e: e.tensor_copy(v[:, t, vc * 512:(vc + 1) * 512], p[:, :]), reads=p.r, writes=[v.r[t]])
                b.barrier()
                with ExitStack() as sr:
                    kTs = [mk(sr, "kT%d" % i, [128, 2, TPB], BF16) for i in range(2)]

                    def load_k(h):
                        row0 = 2048 + h * 256
                        b.dma("pool", kTs[h % 2][:], featT_d[row0:row0 + 256, 0:TPB].rearrange("(j p) n -> p j n", p=128),
                              reads=[featreg], writes=kTs[h % 2].r)

                    kd = [mk(sr, "kd%d" % i, [128, 256], BF16) for i in range(2)]
                    S = mk(sr, "S", [128, 4, 2, 256], F32, nreg=4)
                    utail = mk(sr, "utail", [128, 8, 16], F32)
                    for ct in range(8):
                        b.dma("sp", utail[:, ct, :], featT_d[ct * 128:(ct + 1) * 128, TPB - 16:TPB], reads=[featreg], writes=utail.r)
                    kdn = mk(sr, "kdn", [128, 32], F32)
                    b.dma("sp", kdn[:], c_kdecn, writes=kdn.r)
                    late = []
                    cnt = {"rc": 0, "c": 0, "pt": 0, "it": 0}
                    load_k(0)
                    for h in range(4):
                        if h + 1 < 4:
                            load_k(h + 1)
                        kT = kTs[h % 2]
                        pD = [ps[2 * (h % 2)], ps[2 * (h % 2) + 1]]
                        pend = []
                        for n in range(NPT):
                            c0 = n * 128
                            i2 = cnt["c"] % 2
                            cnt["c"] += 1
                            pt_ = pst[cnt["pt"] % 2]
                            cnt["pt"] += 1
                            for j in range(2):
                                b.op("pe", lambda e, j=j: e.transpose(pt_[:, j * 128:(j + 1) * 128], kT[:, j, c0:c0 + 128], ident[:, :]),
                                     reads=kT.r + ident.r, writes=pt_.r, inc=(j == 1))
                            kd_ = kd[i2]
                            b.op("dve", lambda e: e.tensor_scalar_mul(kd_[:, :], pt_[:, 0:256], kdn[:, n * 4 + h:n * 4 + h + 1]), reads=pt_.r + kdn.r, writes=kd_.r)

                            def acc(kd_=kd_, n=n, h=h, pD=pD):
                                for j in range(2):
                                    b.op("pe", lambda e, j=j: e.matmul(pD[j][:, 0:256], kd_[:, j * 128:(j + 1) * 128], v[:, n, h * 256:(h + 1) * 256], start=(n == 0), stop=(n == NPT - 1)),
                                         reads=kd_.r + [v.r[n]], writes=pD[j].r, inc=True)

                            pend.append(acc)
                            while len(pend) > 1:
                                pend.pop(0)()
                        while pend:
                            pend.pop(0)()
                        for j in range(2):
                            b.op("dve", lambda e, j=j: e.tensor_scalar_mul(S[:, h, j, :], pD[j][:, 0:256], flg[:, 0:1]), reads=pD[j].r + flg.r, writes=[S.r[h]])
                    while late:
                        late.pop(0)()
                    b.op("dve", lambda e: e.tensor_scalar_mul(utail[:], utail[:], flg[:, 0:1]), reads=utail.r + flg.r, writes=utail.r)
                    b.dma("sp", S_d, S[:], reads=S.r, writes=[sdreg])
                    b.dma("sp", utail_d, utail[:], reads=utail.r, writes=[udreg])
            b.barrier()

        def mixer_stage(blk):
            with ExitStack() as sm:
                v = mk(sm, "v", [128, NPT + 1, 1024], BF16, nreg=NPT + 1)
                tcs = [(0, 512), (512, 512), (1024, NS)]
                with ExitStack() as sa:
                    hT = mk(sa, "hT", [128, NKT, TB], BF16, nreg=NPT + 1)
                    norm_stage(sa, hT, 1, 3, 4, blk, stats_done=True)
                    NWI = 3
                    wi = [mk(sa, "wi%d" % i, [128, NKT, 128], BF16) for i in range(NWI)]
                    wv = [mk(sa, "wv%d" % i, [128, NKT, 512], BF16) for i in range(2)]
                    stg = [mk(sa, "stg%d" % i, [128, TB], F32, nreg=3) for i in range(4)]
                    cosT = mk(sa, "cosT", [128, TB], F32)
                    sinT = mk(sa, "sinT", [128, TB], F32)
                    rt = [mk(sa, "rt%d" % i, [128, 512], F32) for i in range(4)]
                    b.dma("sp", cosT[:, 0:TPB], c_cosp[1], writes=cosT.r)
                    b.dma("sp", cosT[:, TPB:TB], c_coss, writes=cosT.r)
                    b.dma("sp", sinT[:, 0:TPB], c_sinp[1], writes=sinT.r)
                    b.dma("sp", sinT[:, TPB:TB], c_sins, writes=sinT.r)
                    cts = list(range(0, 24)) + list(range(32, 40))

                    def load_wi(i):
                        ct = cts[i]
                        b.dma("pool", wi[i % NWI][:], w_in[ct], writes=wi[i % NWI].r)

                    for vc in range(2):
                        b.dma("pool", wv[vc][:], w_inv[vc], writes=wv[vc].r)
                    for i in range(NWI - 1):
                        load_wi(i)
                    pi = 0
                    for i, ct in enumerate(cts):
                        if i + NWI - 1 < len(cts):
                            load_wi(i + NWI - 1)
                        w_ = wi[i % NWI]
                        st_ = stg[i % 4]
                        for ci, (c0, cn) in enumerate(tcs):
                            p = ps[pi % 4]
                            pi += 1
                            for k in range(NKT):
                                b.op("pe", lambda e, k=k: e.matmul(p[:, 0:cn], w_[:, k, :], hT[:, k, c0:c0 + cn], start=(k == 0), stop=(k == NKT - 1)),
                                     reads=w_.r + hT.r, writes=p.r, inc=(k == NKT - 1))
                            if ct >= 32:
                                b.op("act", lambda e: e.activation(st_[:, c0:c0 + cn], p[:, 0:cn], AF.Silu), reads=p.r, writes=[st_.r[ci]])
                            else:
                                b.op("act", lambda e: e.copy(st_[:, c0:c0 + cn], p[:, 0:cn]), reads=p.r, writes=[st_.r[ci]])
                        if ct < 8 or ct >= 32:
                            b.dma("sp", featT_d[ct * 128:(ct + 1) * 128, :], st_[:], reads=st_.r, writes=[featreg])
                        elif ct % 2 == 1:
                            sa_p = stg[(i - 1) % 4]
                            rotary_pair(sa_p, st_, cosT, sinT, rt, tcs)
                            b.dma("sp", featT_d[(ct - 1) * 128:ct * 128, :], sa_p[:], reads=sa_p.r, writes=[featreg])
                            b.dma("sp", featT_d[ct * 128:(ct + 1) * 128, :], st_[:], reads=st_.r, writes=[featreg])
                    for vc in range(2):
                        for t in range(NPT + 1):
                            R = rows_of(t)
                            p = ps[4 + (pi % 2)]
                            pi += 1
                            for k in range(NKT):
                                b.op("pe", lambda e, k=k: e.matmul(p[0:R, :], hT[:, k, t * 128:t * 128 + R], wv[vc][:, k, :], start=(k == 0), stop=(k == NKT - 1)),
                                     reads=wv[vc].r + hT.r, writes=p.r, inc=(k == NKT - 1))
                            if t % 2 == 0:
                                b.op("act", lambda e: e.copy(v[0:R, t, vc * 512:(vc + 1) * 512], p[0:R, :]), reads=p.r, writes=[v.r[t]])
                            else:
                                b.op("dve", lambda e: e.tensor_copy(v[0:R, t, vc * 512:(vc + 1) * 512], p[0:R, :]), reads=p.r, writes=[v.r[t]])
                b.barrier()
                mixT = mk(sm, "mixT", [128, NKT, TB], BF16)
                with ExitStack() as sp_:
                    pw = mk(sp_, "pw", [128, 4, 2, 256], BF16)
                    invc = mk(sp_, "invc", [128, 4, 16], F32)
                    ue_l = [mk(sp_, "ue%d" % i, [128, 2, 16 + TPB], F32) for i in range(2)]
                    wa = mk(sp_, "wa", [128, 2, 16 + TPB], F32)
                    wb = mk(sp_, "wb", [128, 2, 16 + TPB], F32)
                    mT_l = [mk(sp_, "mT%d" % i, [128, 2, TB], BF16) for i in range(2)]
                    ues = mk(sp_, "ues", [128, 2, 19, NSB], F32)
                    sa_ = mk(sp_, "sa_", [128, 2, 19, NSB], F32)
                    sb_ = mk(sp_, "sb_", [128, 2, 19, NSB], F32)
                    t16 = mk(sp_, "t16", [128, 2, 16], F32)
                    sptm = [mk(sp_, "sptm%d" % i, [120, 1024], F32) for i in range(2)]
                    ustm = mk(sp_, "ustm", [NS, 1024], F32)
                    utail = mk(sp_, "utail", [128, 8, 16], F32, nreg=8)
                    b.dma("sp", utail[:], utail_d, reads=[udreg], writes=utail.r)
                    b.dma("pool", pw[:], pool_w.rearrange("g (j p) d -> p g j d", p=128), writes=pw.r)
                    b.dma("sp", invc[:], c_invc[blk], writes=invc.r)
                    for i in range(2):
                        b.dma("sp", sptm[i][:], spool[blk * NSB + i * 8: blk * NSB + (i + 1) * 8].rearrange("b s c -> (b s) c"), writes=sptm[i].r)
                    b.dma("sp", o_pools[blk * NSB:(blk + 1) * NSB, 0:11, :], spool[blk * NSB:(blk + 1) * NSB, 4:15, :])
                    pi = 0
                    for g in range(4):
                        w = 2 ** (g + 1)
                        ue, mT = ue_l[g % 2], mT_l[g % 2]
                        for j in range(2):
                            ct = 2 * g + j
                            b.op("act", lambda e: e.copy(ue[:, j, 0:16], utail[:, ct, :]), reads=[utail.r[ct]], writes=ue.r)
                            b.dma("sp", ue[:, j, 16:16 + TPB], featT_d[ct * 128:(ct + 1) * 128, 0:TPB], reads=[featreg], writes=ue.r)
                            b.dma("sp", ues[:, j, 15:19, :], featT_d[ct * 128:(ct + 1) * 128, TPB:TB].rearrange("p (t b) -> p t b", t=4), reads=[featreg], writes=ues.r)
                            for i in range(2):
                                pt_ = ps[pi % 4]
                                pi += 1
                                b.op("pe", lambda e: e.transpose(pt_[:, 0:120], sptm[i][:, ct * 128:(ct + 1) * 128], identf[0:120, 0:120]),
                                     reads=sptm[i].r + identf.r, writes=pt_.r)
                                dst = ues[:, j, 0:15, i * 8:(i + 1) * 8].rearrange("p s b -> p b s")
                                b.op("act", lambda e: e.copy(dst, pt_[:, 0:120].rearrange("p (b s) -> p b s", b=8)), reads=pt_.r, writes=ues.r)
                        L = 16 + TPB
                        src = ue
                        bufs = [wa, wb]
                        bi_ = 0
                        sh = 1
                        while sh < w:
                            cur = bufs[bi_]
                            bi_ ^= 1
                            b.op("dve", lambda e: e.tensor_tensor(cur[:, :, sh:L], src[:, :, sh:L], src[:, :, 0:L - sh], ALU.add),
                                 reads=src.r, writes=cur.r)
                            src = cur
                            sh *= 2
                        wsum = src
                        b.op("dve", lambda e: e.scalar_tensor_tensor(mT[:, :, 0:TPB], wsum[:, :, 16:L], 1.0 / w, ue[:, :, 16:L], ALU.mult, ALU.subtract),
                             reads=wsum.r + ue.r, writes=mT.r)
                        for j in range(2):
                            b.op("dve", lambda e: e.tensor_tensor(t16[:, j, :], wsum[:, j, 16:32], invc[:, g, :], ALU.mult),
                                 reads=wsum.r + invc.r, writes=t16.r)
                        b.op("dve", lambda e: e.tensor_tensor(mT[:, :, 0:16], t16[:], ue[:, :, 16:32], ALU.subtract),
                             reads=t16.r + ue.r, writes=mT.r)
                        src, cur = ues, sa_
                        sh = 1
                        while sh < w:
                            b.op("pool", lambda e: e.tensor_tensor(cur[:, :, sh:19, :], src[:, :, sh:19, :], src[:, :, 0:19 - sh, :], ALU.add),
                                 reads=src.r, writes=cur.r)
                            nxt = sb_ if cur is sa_ else sa_
                            src, cur = cur, nxt
                            sh *= 2
                        wsum_s = src
                        b.op("dve", lambda e: e.scalar_tensor_tensor(mT[:, :, TPB:TB].rearrange("p j (t b) -> p j t b", t=4), wsum_s[:, :, 15:19, :], 1.0 / w, ues[:, :, 15:19, :], ALU.mult, ALU.subtract),
                             reads=wsum_s.r + ues.r, writes=mT.r)
                        for j in range(2):
                            ct = 2 * g + j
                            b.op("act", lambda e: e.copy(utail[:, ct, :], ue[:, j, TPB:TPB + 16]), reads=ue.r, writes=[utail.r[ct]])
                            pt_ = ps[pi % 4]
                            pi += 1
                            b.op("pe", lambda e: e.transpose(pt_[0:NS, 0:128], ues[:, j, 15:19, :].rearrange("p t b -> p (t b)"), identf[:, :]),
                                 reads=ues.r + identf.r, writes=pt_.r)
                            b.op("act", lambda e: e.copy(ustm[:, ct * 128:(ct + 1) * 128], pt_[0:NS, 0:128]), reads=pt_.r, writes=ustm.r)
                        for dt in range(2):
                            for (c0, cn) in tcs:
                                p = ps[4]
                                pi += 1
                                for j in range(2):
                                    b.op("pe", lambda e, j=j: e.matmul(p[:, 0:cn], pw[:, g, j, dt * 128:(dt + 1) * 128], mT[:, j, c0:c0 + cn], start=(j == 0), stop=(j == 1)),
                                         reads=pw.r + mT.r, writes=p.r, inc=(j == 1))
                                b.op("dve", lambda e: e.tensor_scalar_mul(mixT[:, 2 * g + dt, c0:c0 + cn], p[:, 0:cn], psc[:, 2 * g + dt:2 * g + dt + 1]),
                                     reads=p.r + psc.r, writes=mixT.r)
                    for t in range(4):
                        b.dma("sp", o_pools[blk * NSB:(blk + 1) * NSB, 11 + t, :], ustm[t * NSB:(t + 1) * NSB, :], reads=ustm.r)
                    if blk == NB - 1:
                        pl = mk(sp_, "pl", [16, 1024], F32)
                        for ct in range(8):
                            pt_ = ps[pi % 4]
                            pi += 1
                            b.op("pe", lambda e: e.transpose(pt_[0:16, 0:128], utail[:, ct, :], identf[:, :]), reads=[utail.r[ct]] + identf.r, writes=pt_.r)
                            b.op("act", lambda e: e.copy(pl[:, ct * 128:(ct + 1) * 128], pt_[0:16, 0:128]), reads=pt_.r, writes=pl.r)
                        b.dma("sp", o_poolp, pl[:], reads=pl.r)
                b.barrier()
                with ExitStack() as sr:
                    maskp = mk(sr, "maskp", [128, 4, 128], F32)
                    masks = mk(sr, "masks", [NS, 4, NS], F32)
                    bmask = mk(sr, "bmask", [128, NSB, NS], BF16)
                    rmask = mk(sr, "rmask", [NS, NSB], F32)
                    qTs = [mk(sr, "qT%d" % i, [128, 2, TB], BF16) for i in range(2)]
                    kTs = [mk(sr, "kT%d" % i, [128, 2, TB], BF16) for i in range(2)]

                    def load_qk(h):
                        for (dst, row0) in ((qTs[h % 2], 1024 + h * 256), (kTs[h % 2], 2048 + h * 256)):
                            b.dma("pool", dst[:], featT_d[row0:row0 + 256, :].rearrange("(j p) n -> p j n", p=128),
                                  reads=[featreg], writes=dst.r)

                    gsc = [mk(sr, "gsc%d" % i, [128, 2, 128], F32) for i in range(2)]
                    sT = [mk(sr, "sT%d" % i, [128, 128], BF16) for i in range(2)]
                    kd = [mk(sr, "kd%d" % i, [128, 256], BF16) for i in range(2)]
                    tmpc = [mk(sr, "tmpc%d" % i, [128, 256], F32) for i in range(2)]
                    o_ = [mk(sr, "o%d" % i, [128, 256], F32) for i in range(2)]
                    on = [mk(sr, "on%d" % i, [128, 256], BF16) for i in range(2)]
                    junk = mk(sr, "rjunk", [128, 256], BF16)
                    sso = mk(sr, "sso", [128, 2], F32)
                    qm = mk(sr, "qm", [128, 2, NSB, NS], BF16, nreg=2 * NSB)
                    kdmb = [mk(sr, "kdmb%d" % i, [NS, 256], BF16) for i in range(4)]
                    s0f = [mk(sr, "s0f%d" % i, [128, 2, 256], F32) for i in range(4)]
                    s0b = [mk(sr, "s0b%d" % i, [128, 2, 256], BF16) for i in range(4)]
                    sno = [mk(sr, "sno%d" % i, [128, 2, 256], F32) for i in range(2)]
                    S = mk(sr, "S", [128, 4, 2, 256], F32, nreg=4)
                    Sb = mk(sr, "Sb", [128, 4, 2, 256], BF16, nreg=4)
                    b.dma("sp", S[:], S_d, reads=[sdreg], writes=S.r)
                    for h in range(4):
                        b.op("act", lambda e: e.copy(Sb[:, h, :, :], S[:, h, :, :]), reads=[S.r[h]], writes=[Sb.r[h]])
                    b.dma("sp", maskp[:], c_maskp, writes=maskp.r)
                    b.dma("sp", masks[:], c_masks, writes=masks.r)
                    b.dma("pool", bmask[:], c_bmask, writes=bmask.r)
                    b.dma("sp", rmask[:], c_rmask, writes=rmask.r)
                    cnt = {"rc": 0, "t": 0, "c": 0, "pa": 0, "pt": 0, "s0": 0, "sn": 0}
                    pending = []
                    late_r = mods_emitters(sr, ada_wb, 2 * D // 128, 128, 7 * D, [ps[3]], 3, src_off=4 * D // 128)
                    for h in range(4):
                        gam = GAM[h]
                        if h == 0:
                            load_qk(0)
                        if h + 1 < 4:
                            load_qk(h + 1)
                        qT, kT = qTs[h % 2], kTs[h % 2]
                        for n in range(NPT + 1):
                            R = rows_of(n)
                            c0 = n * 128
                            hh = h if n < NPT else 4 + h
                            i2 = cnt["c"] % 2
                            cnt["c"] += 1
                            g_ = gsc[i2]
                            b.dma("sp", g_[:, :, 0:R], featT_d[4096 + h * 256:4096 + (h + 1) * 256, c0:c0 + R].rearrange("(j p) n -> p j n", p=128),
                                  reads=[featreg], writes=g_.r)
                            pA = ps[cnt["pa"] % 3]
                            cnt["pa"] += 1
                            for j in range(2):
                                b.op("pe", lambda e, j=j: e.matmul(pA[0:R, 0:R], kT[:, j, c0:c0 + R], qT[:, j, c0:c0 + R], start=(j == 0), stop=(j == 1)),
                                     reads=kT.r + qT.r, writes=pA.r, inc=(j == 1))
                            s_ = sT[i2]
                            mk_ = maskp[:, h, :] if n < NPT else masks[:, h, :]
                            b.op("dve", lambda e: e.tensor_tensor(s_[0:R, 0:R], pA[0:R, 0:R], mk_, ALU.mult), reads=pA.r + maskp.r + masks.r, writes=s_.r)
                            pt_ = pst[cnt["pt"] % 2]
                            cnt["pt"] += 1
                            for j in range(2):
                                b.op("pe", lambda e, j=j: e.transpose(pt_[0:R, j * 128:(j + 1) * 128], kT[:, j, c0:c0 + R], ident[:, :]),
                                     reads=kT.r + ident.r, writes=pt_.r, inc=(j == 1))
                            kd_ = kd[i2]
                            b.op("act", lambda e: e.mul(kd_[0:R, :], pt_[0:R, 0:256], kdec[0:R, hh:hh + 1]), reads=pt_.r + kdec.r, writes=kd_.r)
                            pB = ps[cnt["pa"] % 3]
                            cnt["pa"] += 1
                            b.op("pe", lambda e: e.matmul(pB[0:R, 0:256], s_[0:R, 0:R], v[0:R, n, h * 256:(h + 1) * 256], start=True, stop=True),
                                 reads=s_.r + [v.r[n]], writes=pB.r)
                            pC = ps[cnt["pa"] % 3]
                            cnt["pa"] += 1
                            if n < NPT:
                                for j in range(2):
                                    b.op("pe", lambda e, j=j: e.matmul(pC[0:R, 0:256], qT[:, j, c0:c0 + R], Sb[:, h, j, :], start=(j == 0), stop=(j == 1)),
                                         reads=qT.r + [Sb.r[h]], writes=pC.r, inc=(j == 1))
                            else:
                                for j in range(2):
                                    eng = "dve" if j == 0 else "pool"
                                    for bq in range(NSB):
                                        b.op(eng, lambda e: e.tensor_tensor(qm[:, j, bq, :], qT[:, j, c0:c0 + R], bmask[:, bq, :], ALU.mult),
                                             reads=qT.r + bmask.r, writes=[qm.r[j * NSB + bq]])
                                NS0 = len(s0f)

                                def load_s0(bi):
                                    i3 = bi % NS0
                                    src = sret[blk * NSB + bi, h].rearrange("(j p) v -> p j v", p=128)
                                    b.dma("sp", s0f[i3][:], src, writes=s0f[i3].r)
                                    b.dma("pool", s0b[i3][:], src, writes=s0b[i3].r)

                                for bi in range(NS0 - 1):
                                    load_s0(bi)
                                for bi in range(NSB):
                                    if bi + NS0 - 1 < NSB:
                                        load_s0(bi + NS0 - 1)
                                    i3 = bi % NS0
                                    for j in range(2):
                                        b.op("pe", lambda e, j=j: e.matmul(pC[0:R, 0:256], qm[:, j, bi, :], s0b[i3][:, j, :], start=(bi == 0 and j == 0), stop=(bi == NSB - 1 and j == 1)),
                                             reads=[qm.r[j * NSB + bi]] + s0b[i3].r, writes=pC.r, inc=(j == 1))
                                    km_ = kdmb[bi % len(kdmb)]
                                    b.op("dve", lambda e: e.tensor_scalar_mul(km_[:, :], kd_[0:NS, :], rmask[:, bi:bi + 1]),
                                         reads=kd_.r + rmask.r, writes=km_.r)
                                    sn_ = sno[bi % len(sno)]
                                    for j in range(2):
                                        pD = ps[4 + j]
                                        b.op("pe", lambda e: e.matmul(pD[:, 0:256], km_[:, j * 128:(j + 1) * 128], v[0:NS, n, h * 256:(h + 1) * 256], start=True, stop=True),
                                             reads=km_.r + [v.r[n]], writes=pD.r)
                                        b.op("dve", lambda e: e.scalar_tensor_tensor(sn_[:, j, :], s0f[i3][:, j, :], gam ** 4, pD[:, 0:256], ALU.mult, ALU.add),
                                             reads=s0f[i3].r + pD.r, writes=sn_.r)
                                    b.dma("sp", o_rets[blk * NSB + bi, h].rearrange("(j p) v -> p j v", p=128), sn_[:], reads=sn_.r)
                            tc_ = tmpc[i2]
                            b.op("act", lambda e: e.mul(tc_[0:R, :], pC[0:R, 0:256], qdec[0:R, hh:hh + 1]), reads=pC.r + qdec.r, writes=tc_.r)
                            oo = o_[i2]
                            b.op("dve", lambda e: e.tensor_tensor(oo[0:R, :], pB[0:R, 0:256], tc_[0:R, :], ALU.add), reads=pB.r + tc_.r, writes=oo.r)
                            if n < NPT:
                                pD = ps[4 + cnt["c"] % 2]
                                for j in range(2):
                                    b.op("pe", lambda e, j=j: e.matmul(pD[:, j * 256:(j + 1) * 256], kd_[:, j * 128:(j + 1) * 128], v[:, n, h * 256:(h + 1) * 256], start=True, stop=True),
                                         reads=kd_.r + [v.r[n]], writes=pD.r, inc=(j == 1))
                                b.op("dve", lambda e: e.scalar_tensor_tensor(S[:, h, :, :], S[:, h, :, :], gam ** 128, pD[:, :].rearrange("p (j v) -> p j v", j=2), ALU.mult, ALU.add),
                                     reads=[S.r[h]] + pD.r, writes=[S.r[h]])
                                b.op("act", lambda e: e.copy(Sb[:, h, :, :], S[:, h, :, :]), reads=[S.r[h]], writes=[Sb.r[h]])
                                if n == NPT - 1 and blk == NB - 1:
                                    b.dma("sp", o_retp[h].rearrange("(j p) v -> p j v", p=128), S[:, h, :, :], reads=[S.r[h]])
                            b.op("act", lambda e: e.activation(junk[0:R, :], oo[0:R, :], AF.Square, accum_out=sso[0:R, 0:1]), reads=oo.r, writes=junk.r + sso.r)
                            b.op("dve", lambda e: e.tensor_scalar(sso[0:R, 1:2], sso[0:R, 0:1], 1.0 / 256, EPS, ALU.mult, ALU.add), reads=sso.r, writes=sso.r)
                            b.op("act", lambda e: e.sqrt(sso[0:R, 1:2], sso[0:R, 1:2]), reads=sso.r, writes=sso.r)
                            b.op("dve", lambda e: e.reciprocal(sso[0:R, 1:2], sso[0:R, 1:2]), reads=sso.r, writes=sso.r)
                            on_ = on[i2]
                            b.op("dve", lambda e: e.tensor_scalar_mul(on_[0:R, :], oo[0:R, :], sso[0:R, 1:2]), reads=oo.r + sso.r, writes=on_.r)

                            def tail(on_=on_, g_=g_, R=R, c0=c0, h=h):
                                pt2 = pst[cnt["pt"] % 2]
                                cnt["pt"] += 1
                                for j in range(2):
                                    b.op("pe", lambda e, j=j: e.transpose(pt2[:, j * 128:j * 128 + R], on_[0:R, j * 128:(j + 1) * 128], ident[0:R, 0:R]),
                                         reads=on_.r + ident.r, writes=pt2.r, inc=(j == 1))
                                srcp = pt2[:, 0:256].rearrange("p (j n) -> p j n", j=2)[:, :, 0:R]
                                b.op("dve", lambda e: e.tensor_tensor(mixT[:, 8 + 2 * h:10 + 2 * h, c0:c0 + R], srcp, g_[:, :, 0:R], ALU.mult),
                                     reads=pt2.r + g_.r, writes=mixT.r)

                            pending.append(tail)
                            while len(pending) > 1:
                                pending.pop(0)()
                            if late_r and n < NPT:
                                late_r.pop(0)()
                    while pending:
                        pending.pop(0)()
                    while late_r:
                        late_r.pop(0)()
                b.barrier()
                with ExitStack() as so:
                    Gp = mk(so, "G2p", [128, D], F32)
                    Gs = mk(so, "G2s", [128, D], F32)
                    wo = [mk(so, "wo%d" % i, [128, NKT, 512], BF16) for i in range(2)]
                    dt_ = [mk(so, "odt%d" % i, [128, 512], F32) for i in range(2)]
                    for (src, psl) in mod_row_bc(5, 0, blk):
                        b.dma("sp", Gp[psl, :], src, reads=[modreg], writes=Gp.r)
                    for (src, psl) in mod_row_bc(5, 1, blk):
                        b.dma("sp", Gs[psl, :], src, reads=[modreg], writes=Gs.r)
                    b.dma("pool", wo[0][:], w_out[0], writes=wo[0].r)
                    pi = 0
                    for n4 in range(4):
                        if n4 + 1 < 4:
                            b.dma("pool", wo[(n4 + 1) % 2][:], w_out[n4 + 1], writes=wo[(n4 + 1) % 2].r)
                        w_ = wo[n4 % 2]
                        for t in range(NPT + 1):
                            R = rows_of(t)
                            p = ps[pi % 4]
                            d_ = dt_[pi % 2]
                            pi += 1
                            for k in range(NKT):
                                b.op("pe", lambda e, k=k: e.matmul(p[0:R, :], mixT[:, k, t * 128:t * 128 + R], w_[:, k, :], start=(k == 0), stop=(k == NKT - 1)),
                                     reads=mixT.r + w_.r, writes=p.r, inc=(k == NKT - 1))
                            G_ = Gp if t < NPT else Gs
                            b.op("dve", lambda e: e.tensor_tensor(d_[0:R, :], p[0:R, :], G_[0:R, n4 * 512:(n4 + 1) * 512], ALU.mult), reads=p.r + G_.r, writes=d_.r)
                            b.op("pool", lambda e: e.tensor_tensor(x[0:R, t, n4 * 512:(n4 + 1) * 512], x[0:R, t, n4 * 512:(n4 + 1) * 512], d_[0:R, :], ALU.add),
                                 reads=d_.r + [x.r[4 * t + n4]], writes=[x.r[4 * t + n4]])
                    sqj2 = mk(so, "sqj2", [128, D], BF16)
                    batched_rstd(list(range(NPT + 1)), sqj2)
            b.barrier()

        def final_stage(blk, stats_done=False):
            with ExitStack() as sn:
                A = mk(sn, "fA", [128, D], F32)
                yo = [mk(sn, "fy%d" % i, [128, D], F32) for i in range(3)]
                b.dma("sp", A[:], dap(norms, 3 * D, [[0, 128], [1, D]]), writes=A.r)
                if not stats_done:
                    sqj = mk(sn, "sqj", [128, D], BF16)
                    batched_rstd(list(range(NPT + 1)), sqj)
                for t in range(NPT + 1):
                    R = rows_of(t)
                    yt = yo[t % 3]
                    b.op("dve", lambda e: e.scalar_tensor_tensor(yt[0:R, :], x[0:R, t, :], rstd[0:R, t:t + 1], A[0:R, :], ALU.mult, ALU.mult),
                         reads=x.r[4 * t:4 * t + 4] + rstd.r + A.r, writes=yt.r)
                    if t < NPT:
                        b.dma("sp", y_p[blk * TPB + t * 128: blk * TPB + (t + 1) * 128, :], yt[:], reads=yt.r)
                    else:
                        b.dma("sp", y_s[blk * NS:(blk + 1) * NS, :], yt[0:NS, :], reads=yt.r)
            b.barrier()

        def load_x(pre):
            src = xp_pre if pre else xp
            for t in range(NPT):
                b.dma("sp", x[:, t, :], src[t * 128:(t + 1) * 128, :], writes=x.r[4 * t:4 * t + 4])
            if not pre:
                b.dma("sp", x[0:NS, NPT, :], xs[0:NS, :], writes=x.r[4 * NPT:4 * NPT + 4])

        ffn_stage(0, 0, 1, 2, 0, 0, sample=False, late_spec=(0, 32, 3 * D), stats_done=True, next_tiles=list(range(NPT)))
        mixer_lite()
        ffn_stage(0, 0, 1, 2, 0, 0, late_spec=(32, 32, 5 * D), next_tiles=list(range(NPT + 1)))
        mixer_stage(0)
        ffn_stage(1, 6, 7, 8, 2, 0, stats_done=True, next_tiles=list(range(NPT + 1)))
        final_stage(0, stats_done=True)
        b.barrier()
    return nc


def _consts():
    half = 128
    inv = (10000.0 ** (-np.arange(half, dtype=np.float32) / np.float32(half))).astype(np.float32)
    c = {}
    pos_p = np.arange(2 * TPB, dtype=np.float32)
    ang = (pos_p[:, None] * inv[None, :]).astype(np.float32)
    cos_all = np.ascontiguousarray(np.cos(ang).T.reshape(128, 2, TPB).transpose(1, 0, 2)).astype(np.float32)
    sin_all = np.ascontiguousarray(np.sin(ang).T.reshape(128, 2, TPB).transpose(1, 0, 2)).astype(np.float32)
    c["cos_all"], c["sin_all"] = cos_all, sin_all
    tt = np.repeat(np.arange(4), NSB)
    pos_s = (16384.0 + tt).astype(np.float32)
    angs = (pos_s[:, None] * inv[None, :]).astype(np.float32)
    c["c_coss"] = np.ascontiguousarray(np.cos(angs).T).astype(np.float32)
    c["c_sins"] = np.ascontiguousarray(np.sin(angs).T).astype(np.float32)
    c["c_ident"] = np.eye(128, dtype=np.float32)
    gam = np.array(GAM, dtype=np.float64)
    idx = np.arange(128)
    diff = idx[None, :] - idx[:, None]
    maskp = np.zeros((128, 4, 128), np.float64)
    for h in range(4):
        maskp[:, h, :] = np.where(diff >= 0, gam[h] ** np.maximum(diff, 0), 0.0) / 16.0
    c["c_maskp"] = maskp.astype(np.float32)
    ps_ = np.arange(NS)
    tt_, bb_ = ps_ // NSB, ps_ % NSB
    dts = tt_[None, :] - tt_[:, None]
    same = (bb_[None, :] == bb_[:, None])
    masks = np.zeros((NS, 4, NS), np.float64)
    for h in range(4):
        masks[:, h, :] = np.where(same & (dts >= 0), gam[h] ** np.maximum(dts, 0), 0.0) / 16.0
    c["c_masks"] = masks.astype(np.float32)
    qdec = np.zeros((128, 8), np.float64)
    kdec = np.zeros((128, 8), np.float64)
    for h in range(4):
        qdec[:, h] = gam[h] ** (idx + 1.0)
        kdec[:, h] = gam[h] ** (127.0 - idx) / 16.0
        qdec[:NS, 4 + h] = gam[h] ** (tt_ + 1.0)
        kdec[:NS, 4 + h] = gam[h] ** (3.0 - tt_) / 16.0
    kdn = np.zeros((128, 32), np.float64)
    for n in range(8):
        for h in range(4):
            kdn[:, n * 4 + h] = gam[h] ** (127.0 - idx) * gam[h] ** (128.0 * (7 - n)) / 16.0
    c["c_kdecn"] = kdn.astype(np.float32)
    c["c_qdec"] = qdec.astype(np.float32)
    c["c_kdec"] = kdec.astype(np.float32)
    bm = (bb_[None, :] == np.arange(NSB)[:, None]).astype(np.float32)
    c["c_bmask"] = np.ascontiguousarray(np.broadcast_to(bm[None], (128, NSB, NS))).astype(np.float32)
    c["c_rmask"] = np.ascontiguousarray(bm.T).astype(np.float32)
    invc = np.zeros((2, 128, 4, 16), np.float32)
    for par in range(2):
        for g in range(4):
            pos = par * TPB + np.arange(16)
            invc[par, :, g, :] = (1.0 / np.minimum(pos + 1.0, float(2 ** (g + 1))))[None, :]
    c["invc_all"] = invc
    return c


def make_in_maps(inputs):
    g = lambda k: np.asarray(inputs[k], dtype=np.float32)
    cst = _consts()
    cos_all, sin_all, invc_all = cst.pop("cos_all"), cst.pop("sin_all"), cst.pop("invc_all")
    x_prompt, x_sample = g("x_prompt"), g("x_sample")
    c_prompt, c_sample = g("c_prompt"), g("c_sample")
    state_pool, state_ret = g("state_pool")[0], g("state_ret")[0]
    norms = np.stack([g("norm_ffn1")[0], g("norm_mix")[0], g("norm_ffn2")[0], g("norm_final")], 0)

    def ktile(w, cw):
        n = w.shape[1]
        return np.ascontiguousarray(w.reshape(NKT, 128, n // cw, cw).transpose(2, 1, 0, 3))

    w_in_full = g("w_in")[0]
    shared = {
        "ada_w": ktile(np.ascontiguousarray(g("ada_w")[0][:, :3 * D]), 512), "ada_wb": ktile(np.ascontiguousarray(g("ada_w")[0][:, 3 * D:]), 128), "ada_b": g("ada_b"), "norms": np.ascontiguousarray(norms),
        "wg1": ktile(g("ffn1_w_gate")[0], 128), "wu1": ktile(g("ffn1_w_up")[0], 128), "wd1": g("ffn1_w_down")[0],
        "wg2": ktile(g("ffn2_w_gate")[0], 128), "wu2": ktile(g("ffn2_w_up")[0], 128), "wd2": g("ffn2_w_down")[0],
        "w_in": ktile(w_in_full, 128), "w_inv": ktile(np.ascontiguousarray(w_in_full[:, 3072:4096]), 512),
        "pool_w": g("pool_w")[0],
        "pool_sc": np.ascontiguousarray(g("pool_scale")[0].reshape(8, 128).T),
        "w_out": ktile(g("w_out")[0], 512),
    }
    shared.update(cst)
    zeros_pre = np.zeros((TPB, D), np.float32)
    in_maps = []
    for c in range(NCORE):
        sq, par = c // 2, c % 2
        xs = x_sample[c * NSC:(c + 1) * NSC]
        xs = np.ascontiguousarray(xs.transpose(1, 0, 2).reshape(NS, D))
        cT = np.concatenate([c_prompt[sq:sq + 1], c_sample[c * NSC:(c + 1) * NSC]], 0).T
        m = dict(shared)
        m.update({
            "xp": np.ascontiguousarray(x_prompt[sq, par * TPB:(par + 1) * TPB]),
            "xp_pre": np.ascontiguousarray(x_prompt[sq, 0:TPB]) if par == 1 else zeros_pre,
            "flag": np.full((128, 1), float(par), np.float32),
            "xs": xs,
            "cT": np.ascontiguousarray(cT),
            "spool": np.ascontiguousarray(state_pool[c * NSC:(c + 1) * NSC]),
            "sret": np.ascontiguousarray(state_ret[c * NSC:(c + 1) * NSC]),
            "c_cosp": np.ascontiguousarray(np.stack([cos_all[0], cos_all[par]], 0)),
            "c_sinp": np.ascontiguousarray(np.stack([sin_all[0], sin_all[par]], 0)),
            "c_invc": np.ascontiguousarray(invc_all[par:par + 1]),
        })
        in_maps.append(m)
    return in_maps


_NC_CACHE = {}


def kernel(**inputs):
    in_maps = make_in_maps(inputs)
    if "nc" not in _NC_CACHE:
        _NC_CACHE["nc"] = build_program()
    nc = _NC_CACHE["nc"]
    res = run_bass_kernel_spmd(nc, in_maps, core_ids=list(range(NCORE)))
    return assemble(res.results)


def assemble(results):
    y_p = np.stack([np.concatenate([results[2 * s]["y_p"], results[2 * s + 1]["y_p"]], 0) for s in range(4)], 0)
    ys = [results[c]["y_s"].reshape(4, NSB, D).transpose(1, 0, 2) for c in range(NCORE)]
    y_s = np.concatenate(ys, 0)
    pool_p = np.stack([results[2 * s + 1]["o_poolp"][1:16] for s in range(4)], 0)[None]
    ret_p = np.stack([results[2 * s + 1]["o_retp"] for s in range(4)], 0)[None]
    pool_s = np.concatenate([results[c]["o_pools"] for c in range(NCORE)], 0)[None]
    ret_s = np.concatenate([results[c]["o_rets"] for c in range(NCORE)], 0)[None]
    return (y_p.astype(np.float32), y_s.astype(np.float32), pool_p.astype(np.float32),
            ret_p.astype(np.float32), pool_s.astype(np.float32), ret_s.astype(np.float32))
```
